# Optimizing a Trainium2 kernel written in Bass

```python
import math
import jax, jax.numpy as jnp
from jax import lax
import numpy as np

D_MODEL = 1024
BATCH = 2
SEQ = 16384
DEPTH = 2
DEC_BATCH = 8
DEC_SEQ = 16
PAST_LEN = 1024

CHUNK = 64
N_EVEN = (DEPTH + 1) // 2
N_ODD = DEPTH // 2
N_HEADS = 8
KV_HEADS = 2
HEAD_DIM = 64
GROUP = N_HEADS // KV_HEADS
WINDOW = 128
BAND_PREV = WINDOW // CHUNK
ATTN_WIDTH = N_HEADS * HEAD_DIM
KV_WIDTH = KV_HEADS * HEAD_DIM
NUM_BUCKETS = 32
MAX_DISTANCE = 128
POOL_WINDOWS = (2, 4, 8, 16)
POOL_GROUPS = 4
POOL_WIDTH = D_MODEL // 2
POOL_GW = POOL_WIDTH // POOL_GROUPS
POOL_HIST = max(POOL_WINDOWS) - 1
EVEN_IN = ATTN_WIDTH + 2 * KV_WIDTH + POOL_WIDTH
MIX_WIDTH = ATTN_WIDTH + POOL_WIDTH
CONV_WIDTH = 3
CONV_HIST = CONV_WIDTH - 1
D_FF = 4 * D_MODEL
EPS = 1e-6
NEG_INF = -1e30

kernel_name = 'hybrid_swa_pool_shortconv_stream_step'


def rms_norm(x, g):
    xf = x.astype(jnp.float32)
    y = xf * lax.rsqrt(jnp.mean(xf * xf, axis=-1, keepdims=True) + EPS)
    return (y * g.astype(jnp.float32)).astype(x.dtype)


def rel_bucket(rel):
    half = NUM_BUCKETS // 2
    max_exact = half // 2
    n = jnp.abs(rel)
    large = max_exact + (jnp.log(jnp.maximum(n, 1).astype(jnp.float32) / max_exact)
                         / math.log(MAX_DISTANCE / max_exact) * (half - max_exact)).astype(jnp.int32)
    large = jnp.minimum(large, half - 1)
    return jnp.where(rel > 0, half, 0) + jnp.where(n < max_exact, n, large)


def head_bias(rel_bias, rel):
    b = rel_bias[rel_bucket(rel)].astype(jnp.float32)
    q_len, k_len = rel.shape
    return jnp.transpose(b, (2, 0, 1)).reshape(KV_HEADS, GROUP, q_len, k_len)


def sink_softmax(logits, sinks):
    s = sinks.reshape(KV_HEADS, GROUP)[:, :, None, None].astype(jnp.float32)
    m = jnp.maximum(jnp.max(logits, axis=-1, keepdims=True), s)
    e = jnp.exp(logits - m)
    return e / (jnp.sum(e, axis=-1, keepdims=True) + jnp.exp(s - m))


def banded_attention_prompt(q, k, v, sinks, rel_bias):
    B, S = q.shape[:2]
    nc = S // CHUNK
    band = (BAND_PREV + 1) * CHUNK
    qb = q.reshape(B, nc, CHUNK, KV_HEADS, GROUP, HEAD_DIM)
    pad = ((0, 0), (BAND_PREV * CHUNK, 0), (0, 0), (0, 0))
    kc = jnp.pad(k, pad).reshape(B, nc + BAND_PREV, CHUNK, KV_HEADS, HEAD_DIM)
    vc = jnp.pad(v, pad).reshape(B, nc + BAND_PREV, CHUNK, KV_HEADS, HEAD_DIM)
    kb = jnp.concatenate([kc[:, j:j + nc] for j in range(BAND_PREV + 1)], axis=2)
    vb = jnp.concatenate([vc[:, j:j + nc] for j in range(BAND_PREV + 1)], axis=2)
    logits = jnp.einsum('bnqhgd,bnkhd->bnhgqk', qb, kb).astype(jnp.float32) * (HEAD_DIM ** -0.5)
    rel = jnp.arange(band)[None, :] - BAND_PREV * CHUNK - jnp.arange(CHUNK)[:, None]
    bias = head_bias(rel_bias, rel)
    valid = (jnp.arange(nc)[:, None] - BAND_PREV + jnp.arange(band)[None, :] // CHUNK) >= 0
    logits = jnp.where(valid[None, :, None, None, None, :], logits + bias, NEG_INF)
    p = sink_softmax(logits, sinks)
    out = jnp.einsum('bnhgqk,bnkhd->bnqhgd', p.astype(v.dtype), vb)
    return out.reshape(B, S, ATTN_WIDTH)


def window_attention_sample(q, k, v, cache_k, cache_v, sinks, rel_bias):
    B, L = q.shape[:2]
    wc = cache_k.shape[1]
    k_all = jnp.concatenate([cache_k, k], axis=1)
    v_all = jnp.concatenate([cache_v, v], axis=1)
    qh = q.reshape(B, L, KV_HEADS, GROUP, HEAD_DIM)
    logits = jnp.einsum('bqhgd,bkhd->bhgqk', qh, k_all).astype(jnp.float32) * (HEAD_DIM ** -0.5)
    qpos = PAST_LEN + jnp.arange(L)
    kpos = PAST_LEN - wc + jnp.arange(wc + L)
    p = sink_softmax(logits + head_bias(rel_bias, kpos[None, :] - qpos[:, None]), sinks)
    out = jnp.einsum('bhgqk,bkhd->bqhgd', p.astype(v.dtype), v_all)
    return out.reshape(B, L, ATTN_WIDTH), k_all[:, -wc:], v_all[:, -wc:]


def multiscale_pool(u_prev, u, pos0, w_map, scale):
    B, L = u.shape[:2]
    ext = jnp.concatenate([u_prev, u], axis=1)
    cs = jnp.cumsum(ext.astype(jnp.float32), axis=1)
    cs = jnp.pad(cs, ((0, 0), (1, 0), (0, 0)))
    pos = (pos0 + jnp.arange(L)).astype(jnp.float32)[None, :, None]
    uf = u.astype(jnp.float32)
    diffs = []
    for g, w in enumerate(POOL_WINDOWS):
        sl = slice(g * POOL_GW, (g + 1) * POOL_GW)
        s = cs[:, POOL_HIST + 1:, sl] - cs[:, POOL_HIST + 1 - w:POOL_HIST + 1 - w + L, sl]
        cnt = jnp.minimum(pos + 1.0, float(w))
        diffs.append(s / cnt - uf[..., sl])
    d = jnp.stack(diffs, axis=2).astype(u.dtype)
    y = jnp.einsum('blgc,gcd->blgd', d, w_map).reshape(B, L, POOL_WIDTH) * scale
    return y, ext[:, -POOL_HIST:]


def short_conv_mixer(xn, conv_prev, w_in, w_conv, w_out):
    L = xn.shape[1]
    b, c, h = jnp.split(xn @ w_in, 3, axis=-1)
    ext = jnp.concatenate([conv_prev, c * h], axis=1)
    y = sum(w_conv[j] * ext[:, j:j + L] for j in range(CONV_WIDTH))
    return (b * y) @ w_out, ext[:, -CONV_HIST:]


def sq_relu_mlp(xn, w1, w2):
    return jnp.square(jax.nn.relu(xn @ w1)) @ w2


def trunk(x, cache_k, cache_v, st_pool, st_conv, win, pos0, weights):
    (norm_mix, norm_ffn, ffn_w1, ffn_w2, ev_w_in, ev_w_out, q_norm, k_norm, attn_sinks,
     rel_bias, pool_w, pool_scale, conv_w_in, conv_w, conv_w_out) = weights
    prompt = cache_k is None
    B, L = x.shape[:2]
    nk, nv, npool, nconv = [], [], [], []
    for l in range(DEPTH):
        i = l // 2
        xn = rms_norm(x, norm_mix[l])
        if l % 2 == 0:
            q, k, v, u = jnp.split(xn @ ev_w_in[i], [ATTN_WIDTH, ATTN_WIDTH + KV_WIDTH, ATTN_WIDTH + 2 * KV_WIDTH], axis=-1)
            q = rms_norm(q.reshape(B, L, N_HEADS, HEAD_DIM), q_norm[i])
            k = rms_norm(k.reshape(B, L, KV_HEADS, HEAD_DIM), k_norm[i])
            v = v.reshape(B, L, KV_HEADS, HEAD_DIM)
            if prompt:
                a = banded_attention_prompt(q, k, v, attn_sinks[i], rel_bias)
                k_new, v_new = k[:, -win:], v[:, -win:]
                u_prev = jnp.zeros((B, POOL_HIST, POOL_WIDTH), u.dtype)
            else:
                a, k_new, v_new = window_attention_sample(q, k, v, cache_k[i], cache_v[i], attn_sinks[i], rel_bias)
                u_prev = st_pool[i]
            p, pool_new = multiscale_pool(u_prev, u, pos0, pool_w[i], pool_scale[i])
            y = jnp.concatenate([a, p], axis=-1) @ ev_w_out[i]
            nk.append(k_new); nv.append(v_new); npool.append(pool_new)
        else:
            conv_prev = jnp.zeros((B, CONV_HIST, D_MODEL), x.dtype) if prompt else st_conv[i]
            y, conv_new = short_conv_mixer(xn, conv_prev, conv_w_in[i], conv_w[i], conv_w_out[i])
            nconv.append(conv_new)
        x = x + y
        x = x + sq_relu_mlp(rms_norm(x, norm_ffn[l]), ffn_w1[l], ffn_w2[l])
    return x, jnp.stack(nk), jnp.stack(nv), jnp.stack(npool), jnp.stack(nconv)


def setup_inputs(seed: int = 0) -> dict:
    key = jax.random.key(seed)
    ks = jax.random.split(key, 24)
    f32 = jnp.float32
    nrm = lambda k, shape, s: jax.random.normal(k, shape, f32) * s
    win = min(WINDOW, PAST_LEN)
    return {
        'x_prompt': nrm(ks[0], (BATCH, SEQ, D_MODEL), 1.0),
        'x_sample': nrm(ks[1], (DEC_BATCH, DEC_SEQ, D_MODEL), 1.0),
        'cache_k': nrm(ks[2], (N_EVEN, DEC_BATCH, win, KV_HEADS, HEAD_DIM), 1.0),
        'cache_v': nrm(ks[3], (N_EVEN, DEC_BATCH, win, KV_HEADS, HEAD_DIM), 1.0),
        'state_pool': nrm(ks[4], (N_EVEN, DEC_BATCH, POOL_HIST, POOL_WIDTH), 1.0),
        'state_conv': nrm(ks[5], (N_ODD, DEC_BATCH, CONV_HIST, D_MODEL), 1.0),
        'norm_mix': 1.0 + nrm(ks[6], (DEPTH, D_MODEL), 0.05),
        'norm_ffn': 1.0 + nrm(ks[7], (DEPTH, D_MODEL), 0.05),
        'ffn_w1': nrm(ks[8], (DEPTH, D_MODEL, D_FF), D_MODEL ** -0.5),
        'ffn_w2': nrm(ks[9], (DEPTH, D_FF, D_MODEL), D_FF ** -0.5),
        'ev_w_in': nrm(ks[10], (N_EVEN, D_MODEL, EVEN_IN), D_MODEL ** -0.5),
        'ev_w_out': nrm(ks[11], (N_EVEN, MIX_WIDTH, D_MODEL), MIX_WIDTH ** -0.5),
        'q_norm': 1.0 + nrm(ks[12], (N_EVEN, HEAD_DIM), 0.05),
        'k_norm': 1.0 + nrm(ks[13], (N_EVEN, HEAD_DIM), 0.05),
        'attn_sinks': nrm(ks[14], (N_EVEN, N_HEADS), 0.5),
        'rel_bias': nrm(ks[15], (NUM_BUCKETS, N_HEADS), 0.5),
        'pool_w': nrm(ks[16], (N_EVEN, POOL_GROUPS, POOL_GW, POOL_GW), POOL_GW ** -0.5),
        'pool_scale': 1.0 + nrm(ks[17], (N_EVEN, POOL_WIDTH), 0.1),
        'conv_w_in': nrm(ks[18], (N_ODD, D_MODEL, 3 * D_MODEL), D_MODEL ** -0.5),
        'conv_w': nrm(ks[19], (N_ODD, CONV_WIDTH, D_MODEL), 0.5),
        'conv_w_out': nrm(ks[20], (N_ODD, D_MODEL, D_MODEL), D_MODEL ** -0.5),
    }


def reference(x_prompt, x_sample, cache_k, cache_v, state_pool, state_conv, norm_mix, norm_ffn, ffn_w1, ffn_w2,
              ev_w_in, ev_w_out, q_norm, k_norm, attn_sinks, rel_bias, pool_w, pool_scale, conv_w_in, conv_w, conv_w_out):
    weights = (norm_mix, norm_ffn, ffn_w1, ffn_w2, ev_w_in, ev_w_out, q_norm, k_norm, attn_sinks,
               rel_bias, pool_w, pool_scale, conv_w_in, conv_w, conv_w_out)
    win = cache_k.shape[2]
    y_prompt, k_p, v_p, pool_p, conv_p = trunk(x_prompt, None, None, None, None, win, 0, weights)
    y_sample, k_s, v_s, pool_s, conv_s = trunk(x_sample, cache_k, cache_v, state_pool, state_conv, win, PAST_LEN, weights)
    return (y_prompt, y_sample, k_p, v_p, pool_p, conv_p, k_s, v_s, pool_s, conv_s)
```

```python
import numpy as np
from contextlib import ExitStack
import concourse.bass as bass
import concourse.mybir as mybir
from concourse.bass_utils import run_bass_kernel_spmd

F32 = mybir.dt.float32
BF16 = mybir.dt.bfloat16
AF = mybir.ActivationFunctionType
ALU = mybir.AluOpType

NCORE = 8
TOK = 4096
HALO = 192
NCOL = HALO + TOK + 32
NGRP = 9
EPS = 1e-6
NEG = -1.0e4
PW = 2048
NW = 8
INTERLEAVE = True

C_MIX0, C_FFN0, C_MIX1, C_FFN1, C_PSC, C_GQ, C_GK, C_CW, C_SINK, C_NMA, C_NMH, C_INVC, C_ID = 0, 8, 16, 24, 32, 37, 38, 39, 63, 67, 68, 69, 133
NCST = 261
A_KP, A_VP, A_PP, A_CP, A_KS, A_VS, A_PS, A_CS = 0, 128, 256, 316, 332, 460, 588, 648
AUXW = 664
NBIA = 2 * 512 + 256


def piece_table():
    P = []
    def kc8(base):
        return [(kc * 256, 256, None if base is None else base + kc) for kc in range(8)]
    for i in range(5):
        P.append((f"in{i}", kc8(C_MIX0)))
    P.append(("poolw", [(0, 2048, None)]))
    for i in range(4):
        P.append((f"out{i}", [(kc * 256, 256, None if kc < 4 else C_PSC + kc - 4) for kc in range(8)]))
    for i in range(16):
        P.append((f"w1a{i}", kc8(C_FFN0)))
    for i in range(16):
        P.append((f"w2a{i}", [(0, 2048, None)]))
    for f in range(8):
        P.append((f"ca{f}", kc8(C_MIX1)))
        if f % 2 == 0:
            P.append((f"ch{f // 2}", kc8(C_MIX1)))
    for i in range(4):
        P.append((f"co{i}", [(0, 2048, None)]))
    for i in range(16):
        P.append((f"w1b{i}", kc8(C_FFN1)))
    for i in range(16):
        P.append((f"w2b{i}", [(0, 2048, None)]))
    return P


PIECES = piece_table()
NP_ = len(PIECES)
PIDX = {n: i for i, (n, _) in enumerate(PIECES)}


class Res:
    __slots__ = ("name", "w", "r")

    def __init__(self, name):
        self.name = name
        self.w = None
        self.r = {}


class Eng:
    def __init__(self, name, h, key):
        self.name = name
        self.h = h
        self.key = key
        self.seq = 0
        self.known = {}


class Slot:
    __slots__ = ("ap", "res", "t")

    def __init__(self, t, res):
        self.t = t
        self.ap = t
        self.res = res


class Tracker:
    def __init__(self, nc, es):
        self.nc = nc
        self.es = es
        self.sems = {}
        self.snap = {}
        self.dmacnt = {}
        self.PE = self._eng("pe", nc.tensor)
        self.ACT = self._eng("act", nc.scalar)
        self.DVE = self._eng("dve", nc.vector)
        self.POOL = self._eng("pool", nc.gpsimd)
        self.SP = Eng("sp", nc.sync, "q_sp")
        self.nwaits = 0

    def _eng(self, name, h):
        key = "e_" + name
        self.sems[key] = self.es.enter_context(self.nc.semaphore(key))
        return Eng(name, h, key)

    def dsem(self, key):
        if key not in self.sems:
            self.sems[key] = self.es.enter_context(self.nc.semaphore(key))
            self.dmacnt[key] = 0
        return key

    def _deps(self, E, reads, writes, strict=False):
        raw = {}
        war = {}

        def add(d, tgt):
            if d is None:
                return
            k, v = d
            if tgt.get(k, 0) < v:
                tgt[k] = v
        for r in reads:
            add(r.w, raw)
        for w in writes:
            add(w.w, raw)
            for k, v in w.r.items():
                add((k, v), war)
        deps = dict(raw)
        for k, v in war.items():
            if k == E.key and E is not self.POOL and not strict:
                continue
            if deps.get(k, 0) < v:
                deps[k] = v
        if E is self.PE and E.key in deps:
            del deps[E.key]
        for k, v in deps.items():
            if E.known.get(k, 0) >= v:
                continue
            E.h.wait_ge(self.sems[k], v)
            self.nwaits += 1
            E.known[k] = v
            sn = self.snap.get((k, v))
            if sn:
                for k2, v2 in sn.items():
                    if E.known.get(k2, 0) < v2:
                        E.known[k2] = v2

    def op(self, E, fn, reads=(), writes=(), signal=True):
        self._deps(E, reads, writes)
        ins = fn()
        stamp = E.seq + 1
        if signal:
            E.seq = stamp
            ins.then_inc(self.sems[E.key], 1)
            sn = dict(E.known)
            self.snap[(E.key, stamp)] = sn
        for r in reads:
            if r.r.get(E.key, 0) < stamp:
                r.r[E.key] = stamp
        for w in writes:
            w.w = (E.key, stamp)
            w.r = {}
        return ins

    def dma(self, Q, out, in_, semkey, reads=(), writes=()):
        self.dsem(semkey)
        self._deps(Q, reads, writes, strict=True)
        ins = Q.h.dma_start(out=out, in_=in_)
        self.dmacnt[semkey] += 16
        val = self.dmacnt[semkey]
        ins.then_inc(self.sems[semkey], 16)
        self.snap[(semkey, val)] = dict(Q.known)
        for r in reads:
            if r.r.get(semkey, 0) < val:
                r.r[semkey] = val
        for w in writes:
            w.w = (semkey, val)
            w.r = {}

    def barrier(self):
        engs = [self.PE, self.ACT, self.DVE, self.POOL]
        for E in engs + [self.SP]:
            for O in engs:
                if O is E or O.seq == 0:
                    continue
                if E.known.get(O.key, 0) < O.seq:
                    E.h.wait_ge(self.sems[O.key], O.seq)
                    E.known[O.key] = O.seq
            for k, v in self.dmacnt.items():
                if v and E.known.get(k, 0) < v:
                    E.h.wait_ge(self.sems[k], v)
                    E.known[k] = v

    def act(self, out, in_, func, reads, writes, scale=None, bias=None):
        kw = {}
        if scale is not None:
            kw["scale"] = scale
        if bias is not None:
            kw["bias"] = bias
        return self.op(self.ACT, lambda: self.nc.scalar.activation(out=out, in_=in_, func=func, **kw), reads, writes)

    def tt(self, E, out, in0, in1, op, reads, writes):
        return self.op(E, lambda: E.h.tensor_tensor(out=out, in0=in0, in1=in1, op=op), reads, writes)

    def ts(self, E, out, in0, s1, op0, reads, writes, s2=None, op1=None):
        if op1 is None:
            return self.op(E, lambda: E.h.tensor_scalar(out=out, in0=in0, scalar1=s1, scalar2=None, op0=op0), reads, writes)
        return self.op(E, lambda: E.h.tensor_scalar(out=out, in0=in0, scalar1=s1, scalar2=s2, op0=op0, op1=op1), reads, writes)

    def stt(self, out, in0, scalar, in1, op0, op1, reads, writes):
        return self.op(self.DVE, lambda: self.nc.vector.scalar_tensor_tensor(out=out, in0=in0, scalar=scalar, in1=in1, op0=op0, op1=op1), reads, writes)

    def copy(self, E, out, in_, reads, writes):
        if E is self.ACT:
            return self.act(out, in_, AF.Copy, reads, writes)
        return self.op(E, lambda: E.h.tensor_copy(out=out, in_=in_), reads, writes)

    def memset(self, E, ap, val, writes):
        return self.op(E, lambda: E.h.memset(ap, val), (), writes)

    def mm(self, out, lhsT, rhs, start, stop, reads, writes, signal=True):
        return self.op(self.PE, lambda: self.nc.tensor.matmul(out, lhsT=lhsT, rhs=rhs, start=start, stop=stop), reads, writes, signal)

    def tr(self, out, in_, ident, reads, writes):
        return self.op(self.PE, lambda: self.nc.tensor.transpose(out, in_, ident), reads, writes)


class Ring:
    def __init__(self, slots):
        self.slots = slots
        self.i = 0

    def __call__(self):
        s = self.slots[self.i % len(self.slots)]
        self.i += 1
        return s


def build(groups=None, stop=None):
    order = _build(groups, stop, None)
    return _build(groups, stop, order)


def _build(groups, stop, order):
    if groups is None:
        groups = list(range(NGRP))
    nc = bass.Bass("TRN2", target_bir_lowering=False)
    xin = nc.dram_tensor("xin", [128, 8, NCOL], F32, kind="ExternalInput").ap()
    w32 = nc.dram_tensor("w32", [NP_, 128, PW], F32, kind="ExternalInput").ap()
    cst = nc.dram_tensor("cst", [128, NCST], F32, kind="ExternalInput").ap()
    bia = nc.dram_tensor("bia", [128, NBIA], F32, kind="ExternalInput").ap()
    cach = nc.dram_tensor("cach", [128, 3 * 128 + 64 + 16], F32, kind="ExternalInput").ap()
    yT = nc.dram_tensor("yT", [128, 8, NCOL], F32, kind="ExternalOutput").ap()
    aux = nc.dram_tensor("aux", [128, AUXW], F32, kind="ExternalOutput").ap()
    wbf = nc.dram_tensor("wbf", [NP_, 128, PW], BF16, kind="Internal").ap()

    with ExitStack() as es:
        T = Tracker(nc, es)
        PE, ACT, DVE, POOL, SP = T.PE, T.ACT, T.DVE, T.POOL, T.SP

        def sb(name, shape, dt):
            return es.enter_context(nc.sbuf_tensor(name, shape, dt))

        def ps(name, shape, dt):
            return es.enter_context(nc.psum_tensor(name, shape, dt))

        xt = [sb(f"x{i}", [128, 8, 512], F32) for i in range(2)]
        xr = [[Res(f"x{i}_{k}") for k in range(8)] for i in range(2)]
        xb = sb("xb", [128, 8, 512], BF16)
        xbr = [Res(f"xb{k}") for k in range(8)]
        rbc = sb("rbc", [128, 512], F32); rbcr = Res("rbc")
        rr = sb("rr", [128, 512], F32); rrr = Res("rr")
        qn = sb("qn", [128, 4, 512], BF16); qnr = [Res(f"qn{j}") for j in range(4)]
        kT = [sb(f"kT{g}", [128, 768], BF16) for g in range(2)]
        kTh = Res("kTh"); kTc = Res("kTc")
        vT = sb("vT", [128, 64 + 512], BF16); vTh = Res("vTh"); vTc = Res("vTc")
        VA = [sb(f"VA{g}", [128, 6, 128], BF16) for g in range(2)]; VAr = [Res(f"VA{s}") for s in range(6)]
        VB = [sb(f"VB{g}", [128, 4, 128], BF16) for g in range(2)]; VBr = [Res(f"VB{s}") for s in range(4)]
        Vs = [sb(f"Vs{g}", [128, 128], BF16) for g in range(2)]; Vsr = Res("Vs")
        u = sb("u", [128, 4, 528], F32); uh = [Res(f"uh{g}") for g in range(4)]; uc = [Res(f"uc{g}") for g in range(4)]
        dp = sb("dp", [128, 4, 512], BF16); dpr = [Res(f"dp{g}") for g in range(4)]
        pm = sb("pm", [128, 4, 512], BF16); pmr = [Res(f"pm{g}") for g in range(4)]
        byr = dpr + pmr
        by_ap = lambda f: dp[:, f, :] if f < 4 else pm[:, f - 4, :]
        stg = [sb(f"stg{i}", [128, PW], F32) for i in range(2)]; stgr = [Res(f"stg{i}") for i in range(2)]
        aT = sb("aT", [128, 4, 512], BF16); aTr = Res("aT")
        pT1 = [sb("pT1t", [128, 512], BF16), sb("pT1b", [128, 512], BF16)]; pT1r = [Res("pT1t"), Res("pT1b")]
        pT1s = sb("pT1s", [128, 128], BF16); pT1sr = Res("pT1s")
        hid = sb("hid", [128, 32, 512], BF16); hidr = [Res(f"hid{k}") for k in range(32)]
        chh = sb("chh", [128, 8, 2], F32); chhr = [Res(f"chh{k}") for k in range(8)]
        wring = [Slot(sb(f"w{i}", [128, PW], BF16), Res(f"w{i}")) for i in range(NW)]
        fscr = Ring([Slot(sb(f"fs{i}", [128, 528], F32), Res(f"fs{i}")) for i in range(9)])
        pscr = Ring([Slot(sb(f"pscr{i}", [128, 528], F32), Res(f"pscr{i}")) for i in range(4)])
        bscr = Ring([Slot(sb(f"bs{i}", [128, 512], BF16), Res(f"bs{i}")) for i in range(4)])
        cst_sb = sb("cst_sb", [128, NCST], F32); cstr = Res("cst")
        bia_sb = sb("bia_sb", [128, NBIA], F32); biar = Res("bia")
        cach_sb = sb("cach_sb", [128, 3 * 128 + 64 + 16], F32); cachr = Res("cach")
        ckm = [sb(f"ckm{g}", [128, 128], BF16) for g in range(2)]
        cvm = [sb(f"cvm{g}", [128, 128], BF16) for g in range(2)]
        ckvr = Res("ckv")
        aux_sb = sb("aux_sb", [128, AUXW], F32); auxr = Res("aux")
        onesb = sb("onesb", [128, 128], BF16)
        blk = sb("blk", [128, 128], BF16)
        hones = [sb(f"hones{g}", [128, 128], BF16) for g in range(2)]
        identb = sb("identb", [128, 128], BF16)
        esink = sb("esink", [128, 4], F32)
        konst = Res("konst")
        banks = [Slot(ps(f"pb{i}", [128, 512], F32), Res(f"pb{i}")) for i in range(7)]
        tp = ps("tp", [128, 1024], BF16); tpr = [Res(f"tp{i}") for i in range(8)]
        misc = banks[3]

        class Cfg:
            pass
        CF = Cfg()
        CF.mm = Ring([banks[0], banks[1], banks[2]]); CF.misc = banks[3]
        CF.st0 = Ring([banks[4], banks[0]]); CF.st1 = Ring([banks[5], banks[1]]); CF.od = Ring([banks[6], banks[3]])
        CF.qk = [banks[0], banks[1], banks[2], banks[4], banks[5]]
        CI = Cfg()
        CI.mm = Ring([banks[0], banks[1]]); CI.misc = banks[3]
        CI.st0 = Ring([banks[4]]); CI.st1 = Ring([banks[5]]); CI.od = Ring([banks[3]])
        CI.qk = [banks[0], banks[1], banks[4], banks[5], banks[3]]
        down_int = Ring([banks[2], banks[6]])
        st0, st1 = banks[4], banks[5]
        mm3 = Ring([banks[0], banks[1], banks[2]])
        mm6 = Ring([banks[0], banks[1], banks[2], banks[4], banks[5], banks[6]])

        T.dma(SP, cst_sb[:], cst[:, :], "s_setup", writes=[cstr])
        T.dma(SP, bia_sb[:], bia[:, :], "s_setup", writes=[biar])
        T.dma(SP, cach_sb[:], cach[:, :], "s_setup", writes=[cachr])
        tot = T.dmacnt["s_setup"]
        for r_ in (cstr, biar, cachr):
            r_.w = ("s_setup", tot)
        T.memset(DVE, onesb[:], 1.0, [konst])
        T.memset(DVE, blk[:], 0.0, [konst])
        T.memset(DVE, blk[0:64, 0:64], 1.0, [konst])
        T.memset(DVE, blk[64:128, 64:128], 1.0, [konst])
        for g in range(2):
            T.memset(DVE, hones[g][:], 0.0, [konst])
            T.memset(DVE, hones[g][:, g * 64:(g + 1) * 64], 1.0, [konst])
        T.copy(DVE, identb[:], cst_sb[:, C_ID:C_ID + 128], [cstr], [konst])
        T.act(esink[:], cst_sb[:, C_SINK:C_SINK + 4], AF.Exp, [cstr], [konst])
        for g in range(2):
            T.memset(POOL, kT[g][:], 0.0, [kTh, kTc])
            T.memset(POOL, VA[g][:], 0.0, VAr)
            T.memset(POOL, VB[g][:], 0.0, VBr)
            T.memset(POOL, Vs[g][:], 0.0, [Vsr])
            T.memset(POOL, ckm[g][:], 0.0, [ckvr])
            T.memset(POOL, cvm[g][:], 0.0, [ckvr])
            T.memset(POOL, pT1[g][:], 0.0, [pT1r[g]])
        T.memset(POOL, pT1s[:], 0.0, [pT1sr])
        T.memset(POOL, vT[:], 0.0, [vTh, vTc])
        T.memset(POOL, u[:], 0.0, uh + uc)
        T.memset(POOL, chh[:], 0.0, chhr)
        T.memset(POOL, aT[:], 0.0, [aTr])
        T.memset(POOL, aux_sb[:], 0.0, [auxr])
        CK, CVT, CV, SPT, SCT = 0, 128, 256, 384, 448
        for g in range(2):
            T.copy(DVE, ckm[g][g * 64:(g + 1) * 64, :], cach_sb[g * 64:(g + 1) * 64, CK:CK + 128], [cachr], [ckvr])
            T.copy(DVE, cvm[g][:, g * 64:(g + 1) * 64], cach_sb[:, CV + g * 64:CV + (g + 1) * 64], [cachr], [ckvr])
        T.copy(DVE, aux_sb[:, A_KS:A_KS + 112], cach_sb[:, CK + 16:CK + 128], [cachr], [auxr])
        T.copy(DVE, aux_sb[:, A_VS:A_VS + 112], cach_sb[:, CVT + 16:CVT + 128], [cachr], [auxr])

        wbfr = [Res(f"wbf{p}") for p in range(NP_)]

        NSTG = 4

        def stage_ap(i):
            if i < 2:
                return xt[1][:, 4 * i:4 * i + 4, :].rearrange("p a b -> p (a b)")
            return stg[i - 2][:, :]

        stgh = [[Res(f"stg{i}h{h}") for h in range(2)] for i in range(2)]

        def stage_res_half(i, h):
            return xr[1][4 * i + 2 * h:4 * i + 2 * h + 2] if i < 2 else [stgh[i - 2][h]]

        def stage_res(i):
            return stage_res_half(i, 0) + stage_res_half(i, 1)

        class WS:
            def __init__(self):
                self.issued = 0
                self.loaded = 0
                self.taken = 0
                self.done = 0
                self.cnt = 0
                self.nconv = 0
                self.rec = [] if order is None else None
                self.order = order or []
                self.total = len(self.order)
                self.first = {}
                for i_, p_ in enumerate(self.order):
                    self.first.setdefault(p_, i_)
                assert order is None or sorted(self.first, key=self.first.get) == list(range(NP_))

            def _convert(self, p, slot):
                sf = stage_ap(p % NSTG)
                sres = stage_res(p % NSTG)
                wsl = wring[slot]
                for (a_, ln_, col) in PIECES[p][1]:
                    E = DVE if self.cnt % 2 == 0 else ACT
                    self.cnt += 1
                    if col is None:
                        T.copy(E, wsl.ap[:, a_:a_ + ln_], sf[:, a_:a_ + ln_], sres, [wsl.res])
                    elif E is DVE:
                        T.ts(DVE, wsl.ap[:, a_:a_ + ln_], sf[:, a_:a_ + ln_], cst_sb[:, col:col + 1], ALU.mult, sres + [cstr], [wsl.res])
                    else:
                        T.act(wsl.ap[:, a_:a_ + ln_], sf[:, a_:a_ + ln_], AF.Copy, sres + [cstr], [wsl.res], scale=cst_sb[:, col:col + 1])
                T.dma(SP, wbf[p], wsl.ap[:, :], f"s_wo{slot}", reads=[wsl.res], writes=[wbfr[p]])

            def _stage_loads(self):
                while self.loaded < NP_ and self.loaded < self.nconv + NSTG:
                    p = self.loaded
                    si_ = p % NSTG
                    T.dma(SP, stage_ap(si_)[:, 0:PW // 2], w32[p][:, 0:PW // 2], f"s_stg{si_}a", writes=stage_res_half(si_, 0))
                    T.dma(ACT, stage_ap(si_)[:, PW // 2:PW], w32[p][:, PW // 2:PW], f"s_stg{si_}b", writes=stage_res_half(si_, 1))
                    self.loaded += 1

            def _pump(self):
                self._stage_loads()
                while self.issued < self.total and self.issued < self.done + NW:
                    s_ = self.issued
                    p_ = self.order[s_]
                    if self.first[p_] == s_:
                        assert p_ == self.nconv
                        self._convert(p_, s_ % NW)
                        self.nconv += 1
                        self.issued += 1
                        self._stage_loads()
                    else:
                        sl = wring[s_ % NW]
                        T.dma(SP, sl.ap[:, :], wbf[p_], f"s_w{s_ % NW}", reads=[wbfr[p_]], writes=[sl.res])
                        self.issued += 1

            def next(self, name):
                if self.rec is not None:
                    self.rec.append(PIDX[name])
                    self.taken += 1
                    return wring[(self.taken - 1) % NW]
                self._pump()
                s_ = self.taken
                assert PIECES[self.order[s_]][0] == name, (PIECES[self.order[s_]][0], name)
                assert s_ < self.issued
                self.taken += 1
                return wring[s_ % NW]

            def release(self, n=1):
                if self.rec is not None:
                    return
                self.done += n
                self._pump()
        W = WS()

        def load_x(gi):
            n0 = 512 * gi
            NC = 512 if gi < 8 else 224
            s = gi % 2
            T.dma(SP, xt[s][:, :, 0:NC], xin[:, :, n0:n0 + NC], f"s_x{s}", writes=xr[s])

        def norm(X, Xr, NC, want_r, want_rr, misc=misc):
            for kc in range(8):
                sq = bscr()
                T.act(sq.ap[:, :NC], X[:, kc, :NC], AF.Square, [Xr[kc]], [sq.res])
                T.copy(DVE, xb[:, kc, :NC], X[:, kc, :NC], [Xr[kc]], [xbr[kc]])
                T.mm(misc.ap[:, :NC], onesb[:, :], sq.ap[:, :NC], kc == 0, kc == 7, [sq.res, konst], [misc.res])
            ln = fscr()
            T.act(ln.ap[:, :NC], misc.ap[:, :NC], AF.Ln, [misc.res], [ln.res], scale=1.0 / 1024, bias=EPS)
            if want_r:
                T.act(rbc[:, :NC], ln.ap[:, :NC], AF.Exp, [ln.res], [rbcr], scale=-0.5)
            if want_rr:
                T.act(rr[:, :NC], ln.ap[:, :NC], AF.Exp, [ln.res], [rrr], scale=-1.0)

        def proj8(bank, wv, c0, rhs_fn, NC, wres):
            for kc in range(8):
                rap, rres = rhs_fn(kc)
                T.mm(bank.ap[:, :NC], wv[:, kc, c0:c0 + 128], rap, kc == 0, kc == 7, [wres, rres], [bank.res], signal=(kc == 7))

        def ffn(X, Xr, NC, tag, on_up=None, on_down=None, dring=None):
            norm(X, Xr, NC, False, True)
            for pp in range(16):
                w = W.next(f"w1{tag}{pp}")
                wv = w.ap.rearrange("p (k n) -> p k n", k=8)
                for t in range(2):
                    ht = 2 * pp + t
                    bank = mm6()
                    proj8(bank, wv, t * 128, lambda kc: (xb[:, kc, :NC], xbr[kc]), NC, w.res)
                    a = fscr()
                    T.act(a.ap[:, :NC], bank.ap[:, :NC], AF.Relu, [bank.res], [a.res])
                    T.stt(hid[:, ht, :NC], bank.ap[:, :NC], 0.0, a.ap[:, :NC], ALU.max, ALU.mult, [bank.res, a.res], [hidr[ht]])
                W.release()
                if pp == 7 and on_up is not None:
                    on_up()
            for n in range(8):
                bank = mm6() if dring is None else dring()
                for hh in range(2):
                    w = W.next(f"w2{tag}{2 * n + hh}")
                    wv = w.ap.rearrange("p (k n) -> p k n", k=16)
                    for kl in range(16):
                        kc = 16 * hh + kl
                        T.mm(bank.ap[:, :NC], wv[:, kl, :], hid[:, kc, :NC], kc == 0, kc == 31, [w.res, hidr[kc]], [bank.res], signal=(kl == 15))
                    if hh == 0 and on_down is not None:
                        W.release(1)
                        on_down(n)
                t_ = fscr()
                T.tt(DVE, t_.ap[:, :NC], bank.ap[:, :NC], rr[:, :NC], ALU.mult, [bank.res, rrr], [t_.res])
                addE = DVE if (on_down is not None or n % 2 == 1) else POOL
                T.tt(addE, X[:, n, :NC], X[:, n, :NC], t_.ap[:, :NC], ALU.add, [Xr[n], t_.res], [Xr[n]])
                W.release(1 if on_down is not None else 2)
                if on_down is not None:
                    on_down(n)

        def attn_scores(C, qc0, nq, k0, k0res, v0, v0res, b0, b0res, lo, hi, k1, v1, v1res, b1, pbuf, pres):
            Wd = 4 * nq
            st0, st1 = C.st0(), C.st1()
            qres = qnr + [kTh, kTc]
            q3 = qn[:, 0:4, qc0:qc0 + nq]
            for g in range(2):
                T.mm(st0.ap[:, g * Wd:(g + 1) * Wd].rearrange("p (j q) -> p j q", j=4), k0(g), q3, True, True, qres + k0res, [st0.res], signal=(g == 1))
            for g in range(2):
                T.mm(st1.ap[:, g * Wd:(g + 1) * Wd].rearrange("p (j q) -> p j q", j=4), k1(g), q3, True, True, qres, [st1.res], signal=(g == 1))
            s0 = fscr()
            T.stt(s0.ap[:, :2 * Wd], st0.ap[:, :2 * Wd], 0.125, b0, ALU.mult, ALU.add, [st0.res, b0res], [s0.res])
            p0 = bscr()
            T.act(p0.ap[:, :2 * Wd], s0.ap[:, :2 * Wd], AF.Exp, [s0.res], [p0.res])
            s1 = fscr()
            T.stt(s1.ap[lo:hi, :2 * Wd], st1.ap[lo:hi, :2 * Wd], 0.125, b1[lo:hi, :], ALU.mult, ALU.add, [st1.res, biar], [s1.res])
            T.act(pbuf[lo:hi, :2 * Wd], s1.ap[lo:hi, :2 * Wd], AF.Exp, [s1.res], [pres])
            return (C, qc0, nq, v0, v0res, v1, v1res, pbuf, pres, p0)

        def attn_out(state):
            C, qc0, nq, v0, v0res, v1, v1res, pbuf, pres, p0 = state
            Wd = 4 * nq
            od = C.od()
            mms = [(v0(g), p0.ap[:, g * Wd:(g + 1) * Wd], [p0.res] + v0res) for g in range(2)] + \
                  [(v1(g), pbuf[:, g * Wd:(g + 1) * Wd], [pres] + v1res) for g in range(2)]
            for i, (l_, r_, rs) in enumerate(mms):
                T.mm(od.ap[:, 0:Wd], l_, r_, i == 0, i == 3, rs, [od.res], signal=False)
            dms = [(hones[g][:, :], p0.ap[:, g * Wd:(g + 1) * Wd], [p0.res, konst]) for g in range(2)] + \
                  [(hones[g][:, :], pbuf[:, g * Wd:(g + 1) * Wd], [pres, konst]) for g in range(2)]
            for i, (l_, r_, rs) in enumerate(dms):
                T.mm(od.ap[:, 256:256 + Wd], l_, r_, i == 0, i == 3, rs, [od.res], signal=(i == 3))
            ds = fscr()
            d3 = ds.ap[:, 0:Wd].rearrange("p (j q) -> p j q", j=4)
            T.tt(DVE, d3, od.ap[:, 256:256 + Wd].rearrange("p (j q) -> p j q", j=4),
                 esink[:, 0:4].unsqueeze(2).broadcast_to([128, 4, nq]), ALU.add, [od.res, konst], [ds.res])
            T.act(ds.ap[:, 0:Wd], ds.ap[:, 0:Wd], AF.Ln, [ds.res], [ds.res])
            T.act(ds.ap[:, 0:Wd], ds.ap[:, 0:Wd], AF.Exp, [ds.res], [ds.res], scale=-1.0)
            T.tt(DVE, aT[:, 0:4, qc0:qc0 + nq], od.ap[:, 0:Wd].rearrange("p (j q) -> p j q", j=4), d3, ALU.mult, [od.res, ds.res], [aTr])

        def layer0(gi, X, Xr, NC, C):
            last = gi == 8
            norm(X, Xr, NC, True, False, C.misc)
            yield
            xrhs = lambda kc: (xb[:, kc, :NC], xbr[kc])
            qfs = []
            for half in range(2):
                w = W.next(f"in{half}")
                wv = w.ap.rearrange("p (k n) -> p k n", k=8)
                for t in range(2):
                    bank = C.mm()
                    proj8(bank, wv, t * 128, xrhs, NC, w.res)
                    qf = fscr()
                    T.tt(DVE, qf.ap[:, :NC], bank.ap[:, :NC], rbc[:, :NC], ALU.mult, [bank.res, rbcr], [qf.res])
                    qfs.append(qf)
                W.release()
                yield
            w = W.next("in2")
            wv = w.ap.rearrange("p (k n) -> p k n", k=8)
            bank = C.mm()
            proj8(bank, wv, 0, xrhs, NC, w.res)
            kf = fscr()
            T.tt(DVE, kf.ap[:, :NC], bank.ap[:, :NC], rbc[:, :NC], ALU.mult, [bank.res, rbcr], [kf.res])
            bank = C.mm()
            proj8(bank, wv, 128, xrhs, NC, w.res)
            T.tt(DVE, vT[:, 64:64 + NC], bank.ap[:, :NC], rbc[:, :NC], ALU.mult, [bank.res, rbcr], [vTc])
            if last:
                T.tt(DVE, aux_sb[:, A_VP:A_VP + 128], bank.ap[:, 64:192], rbc[:, 64:192], ALU.mult, [bank.res, rbcr], [auxr])
                T.tt(DVE, aux_sb[:, A_VS + 112:A_VS + 128], bank.ap[:, 208:224], rbc[:, 208:224], ALU.mult, [bank.res, rbcr], [auxr])
            W.release()
            yield
            for half in range(2):
                w = W.next(f"in{3 + half}")
                wv = w.ap.rearrange("p (k n) -> p k n", k=8)
                for t in range(2):
                    gq_ = 2 * half + t
                    bank = C.mm()
                    proj8(bank, wv, t * 128, xrhs, NC, w.res)
                    T.tt(DVE, u[:, gq_, 16:16 + NC], bank.ap[:, :NC], rbc[:, :NC], ALU.mult, [bank.res, rbcr], [uc[gq_]])
                    if last:
                        T.copy(DVE, u[:, gq_, 16 + 192:16 + 208], cach_sb[:, SPT + 16 * gq_:SPT + 16 * gq_ + 16], [cachr], [uc[gq_]])
                W.release()
                yield
            def qk_sq(qf):
                sq = bscr()
                T.act(sq.ap[:, :NC], qf.ap[:, :NC], AF.Square, [qf.res], [sq.res])
                return sq

            def qk_mm(sq, qb):
                T.mm(qb.ap[:, :NC], blk[:, :], sq.ap[:, :NC], True, True, [sq.res, konst], [qb.res])

            def qk_fin(qf, qb, outs, gcol):
                T.act(qb.ap[:, :NC], qb.ap[:, :NC], AF.Ln, [qb.res], [qb.res], scale=1.0 / 64, bias=EPS)
                T.act(qb.ap[:, :NC], qb.ap[:, :NC], AF.Exp, [qb.res], [qb.res], scale=-0.5)
                for (oap, lo, hi, c0, c1, ores) in outs:
                    T.stt(oap, qf.ap[lo:hi, c0:c1], cst_sb[lo:hi, gcol:gcol + 1], qb.ap[lo:hi, c0:c1], ALU.mult, ALU.mult, [qf.res, qb.res, cstr], ores)
            qbanks = C.qk
            sqs = [qk_sq(qf_) for qf_ in qfs]
            qk_mm(sqs[0], qbanks[0])
            sqk = qk_sq(kf)
            for j in range(1, 4):
                qk_mm(sqs[j], qbanks[j])
            qk_mm(sqk, qbanks[4])
            for j in range(4):
                qk_fin(qfs[j], qbanks[j], [(qn[:, j, :NC], 0, 128, 0, NC, [qnr[j]])], C_GQ)
            kouts = [(kT[0][0:64, 128:128 + NC], 0, 64, 0, NC, [kTc]), (kT[1][64:128, 128:128 + NC], 64, 128, 0, NC, [kTc])]
            if last:
                kouts.append((aux_sb[:, A_KP:A_KP + 128], 0, 128, 64, 192, [auxr]))
                kouts.append((aux_sb[:, A_KS + 112:A_KS + 128], 0, 128, 208, 224, [auxr]))
            qk_fin(kf, qbanks[4], kouts, C_GK)
            nA = 4 if not last else 2
            tps = [vTc, vTh, konst]
            for m in range(nA):
                T.tr(tp[:, m * 128:(m + 1) * 128], vT[:, 64 + 128 * m:64 + 128 * (m + 1)], identb[:, :], tps, [tpr[0]])
                T.tr(tp[:, (4 + m) * 128:(5 + m) * 128], vT[:, 128 * m:128 * (m + 1)], identb[:, :], tps, [tpr[0]])
            if last:
                T.tr(tp[:, 2 * 128:3 * 128], vT[:, 64 + 208:64 + 336], identb[:, :], tps, [tpr[0]])
            for m in range(nA):
                TA = 4 * gi + m
                T.copy(ACT, VA[0][:, TA % 6, 0:64], tp[:, m * 128:m * 128 + 64], [tpr[0]], [VAr[TA % 6]])
                T.copy(DVE, VA[1][:, TA % 6, 64:128], tp[:, m * 128 + 64:(m + 1) * 128], [tpr[0]], [VAr[TA % 6]])
                T.copy(ACT, VB[0][:, TA % 4, 0:64], tp[:, (4 + m) * 128:(4 + m) * 128 + 64], [tpr[0]], [VBr[TA % 4]])
                T.copy(DVE, VB[1][:, TA % 4, 64:128], tp[:, (4 + m) * 128 + 64:(5 + m) * 128], [tpr[0]], [VBr[TA % 4]])
            if last:
                T.copy(ACT, Vs[0][:, 0:64], tp[:, 256:320], [tpr[0]], [Vsr])
                T.copy(DVE, Vs[1][:, 64:128], tp[:, 320:384], [tpr[0]], [Vsr])
            L = 16 + NC
            for g in range(4):
                Ug = u[:, g, :]
                cur = Ug
                cres = [uh[g], uc[g]]
                shift = 1
                for stg_ in range(g + 1):
                    nx = pscr()
                    lo = 2 ** (stg_ + 1) - 1
                    T.tt(POOL, nx.ap[:, lo:L], cur[:, lo:L], cur[:, lo - shift:L - shift], ALU.add, cres, [nx.res])
                    cur = nx.ap
                    cres = [nx.res]
                    shift *= 2
                wv_ = float(2 ** (g + 1))
                sc = pscr()
                T.ts(POOL, sc.ap[:, 16:L], cur[:, 16:L], 1.0 / wv_, ALU.mult, cres, [sc.res], s2=0.0, op1=ALU.add)
                if gi == 0:
                    T.tt(POOL, sc.ap[:, 208:224], cur[:, 208:224], cst_sb[:, C_INVC + 16 * g:C_INVC + 16 * g + 16], ALU.mult, cres + [cstr], [sc.res])
                T.tt(POOL, dp[:, g, :NC], sc.ap[:, 16:L], Ug[:, 16:L], ALU.subtract, [sc.res, uc[g]], [dpr[g]])
                if last:
                    T.copy(DVE, aux_sb[:, A_PP + 15 * g:A_PP + 15 * g + 15], Ug[:, 16 + 177:16 + 192], [uc[g]], [auxr])
                    T.copy(DVE, aux_sb[:, A_PS + 15 * g:A_PS + 15 * g + 15], Ug[:, 16 + 209:16 + 224], [uc[g]], [auxr])
            yield
            nch = 8 if not last else 3
            prev = None
            for lc in range(nch):
                c = 8 * gi + lc
                if c < 2:
                    continue
                ev = (c % 2 == 0)
                k0 = lambda g, lc=lc: kT[g][:, 64 * lc:64 * lc + 128]
                if ev:
                    sA = ((c - 2) // 2) % 6
                    v0 = lambda g, sA=sA: VA[g][:, sA, :]
                    v0res = [VAr[sA]]
                else:
                    sB = ((c - 1) // 2) % 4
                    v0 = lambda g, sB=sB: VB[g][:, sB, :]
                    v0res = [VBr[sB]]
                if gi == 0 and c in (3, 4):
                    mb = fscr()
                    mcol = C_NMA if c == 3 else C_NMH
                    T.ts(DVE, mb.ap[:, 0:512], bia_sb[:, 0:512], cst_sb[:, mcol:mcol + 1], ALU.add, [cstr, biar], [mb.res])
                    b0, b0res = mb.ap[:, 0:512], mb.res
                else:
                    b0, b0res = bia_sb[:, 0:512], biar
                if ev:
                    lo, hi = 0, 64
                    k1 = lambda g, lc=lc: kT[g][:, 128 + 64 * lc:128 + 64 * lc + 128]
                    s1A = (c // 2) % 6
                    pb_, pr_ = pT1[0], pT1r[0]
                else:
                    lo, hi = 64, 128
                    k1 = lambda g, lc=lc: kT[g][:, 128 + 64 * (lc - 1):128 + 64 * (lc - 1) + 128]
                    s1A = ((c - 1) // 2) % 6
                    pb_, pr_ = pT1[1], pT1r[1]
                v1 = lambda g, s1A=s1A: VA[g][:, s1A, :]
                st_ = attn_scores(C, 64 * lc, 64, k0, [], v0, v0res, b0, b0res, lo, hi, k1, v1, [VAr[s1A]], bia_sb[:, 512:1024], pb_, pr_)
                if prev is not None:
                    attn_out(prev)
                prev = st_
                yield
            if last:
                st_ = attn_scores(C, 208, 16, lambda g: ckm[g][:, :], [ckvr], lambda g: cvm[g][:, :], [ckvr], bia_sb[:, 1024:1152], biar, 0, 16,
                                  lambda g: kT[g][:, 128 + 208:128 + 336], lambda g: Vs[g][:, :], [Vsr], bia_sb[:, 1152:1280], pT1s, pT1sr)
                attn_out(prev)
                prev = st_
            if prev is not None:
                attn_out(prev)
            w = W.next("poolw")
            wv = w.ap[:, 0:512].rearrange("p (g d) -> p g d", g=4)
            for g in range(4):
                bank = C.mm()
                T.mm(bank.ap[:, :NC], wv[:, g, :], dp[:, g, :NC], True, True, [w.res, dpr[g]], [bank.res])
                T.copy(ACT, pm[:, g, :NC], bank.ap[:, :NC], [bank.res], [pmr[g]])
            W.release()
            if not last:
                for g in range(2):
                    T.copy(POOL, kT[g][:, 0:128], kT[g][:, NC:NC + 128], [kTc], [kTh])
                T.copy(POOL, vT[:, 0:64], vT[:, NC:NC + 64], [vTc], [vTh])
                for g in range(4):
                    T.copy(POOL, u[:, g, 0:16], u[:, g, NC:NC + 16], [uc[g]], [uh[g]])
            yield
            for pp in range(4):
                w = W.next(f"out{pp}")
                wv = w.ap.rearrange("p (k n) -> p k n", k=8)
                for t in range(2):
                    n = 2 * pp + t
                    bank = C.mm()
                    proj8(bank, wv, t * 128, lambda kc: (aT[:, kc, :NC], aTr) if kc < 4 else (pm[:, kc - 4, :NC], pmr[kc - 4]), NC, w.res)
                    T.tt(DVE, X[:, n, :NC], bank.ap[:, :NC], X[:, n, :NC], ALU.add, [bank.res, Xr[n]], [Xr[n]])
                W.release()
                if pp == 1:
                    yield

        def layer1(gi, X, Xr, NC):
            last = gi == 8
            norm(X, Xr, NC, True, True)
            xrhs = lambda kc: (xb[:, kc, :NC], xbr[kc])
            wh = None
            pendB = None

            def stageB(f, ch, tb, y0, cw2):
                T.stt(y0.ap[:, :NC], ch.ap[:, 2:2 + NC], cw2, y0.ap[:, :NC], ALU.mult, ALU.add, [ch.res, y0.res, cstr], [y0.res])
                T.tt(DVE, by_ap(f)[:, :NC], y0.ap[:, :NC], tb.ap[:, :NC], ALU.mult, [y0.res, tb.res], [byr[f]])
            for f in range(8):
                wa = W.next(f"ca{f}")
                if f % 2 == 0:
                    wh = W.next(f"ch{f // 2}")
                wav = wa.ap.rearrange("p (k n) -> p k n", k=8)
                whv = wh.ap.rearrange("p (k n) -> p k n", k=8)
                bB, bC, bH = mm6(), mm6(), mm6()
                proj8(bC, wav, 128, xrhs, NC, wa.res)
                proj8(bH, whv, (f % 2) * 128, xrhs, NC, wh.res)
                proj8(bB, wav, 0, xrhs, NC, wa.res)
                if f % 2 == 0:
                    W.release(1)
                else:
                    W.release(2)
                ch = fscr()
                T.copy(POOL, ch.ap[:, 0:2], chh[:, f, :], [chhr[f]], [ch.res])
                T.tt(DVE, ch.ap[:, 2:2 + NC], bC.ap[:, :NC], rr[:, :NC], ALU.mult, [bC.res, rrr], [ch.res])
                T.tt(DVE, ch.ap[:, 2:2 + NC], bH.ap[:, :NC], ch.ap[:, 2:2 + NC], ALU.mult, [bH.res, ch.res], [ch.res])
                tb = fscr()
                T.tt(DVE, tb.ap[:, :NC], bB.ap[:, :NC], rbc[:, :NC], ALU.mult, [bB.res, rbcr], [tb.res])
                if last:
                    T.copy(POOL, ch.ap[:, 2 + 206:2 + 208], cach_sb[:, SCT + 2 * f:SCT + 2 * f + 2], [cachr], [ch.res])
                    T.copy(POOL, aux_sb[:, A_CP + 2 * f:A_CP + 2 * f + 2], ch.ap[:, 2 + 190:2 + 192], [ch.res], [auxr])
                    T.copy(POOL, aux_sb[:, A_CS + 2 * f:A_CS + 2 * f + 2], ch.ap[:, 2 + 222:2 + 224], [ch.res], [auxr])
                else:
                    T.copy(POOL, chh[:, f, :], ch.ap[:, NC:NC + 2], [ch.res], [chhr[f]])
                cw = lambda j, f=f: cst_sb[:, C_CW + 8 * j + f:C_CW + 8 * j + f + 1]
                y0 = fscr()
                T.act(y0.ap[:, :NC], ch.ap[:, 0:NC], AF.Copy, [ch.res, cstr], [y0.res], scale=cw(0))
                y1 = fscr()
                T.act(y1.ap[:, :NC], ch.ap[:, 1:1 + NC], AF.Copy, [ch.res, cstr], [y1.res], scale=cw(1))
                T.tt(POOL, y0.ap[:, :NC], y0.ap[:, :NC], y1.ap[:, :NC], ALU.add, [y0.res, y1.res], [y0.res])
                if pendB is not None:
                    stageB(*pendB)
                pendB = (f, ch, tb, y0, cw(2))
            stageB(*pendB)
            cow = []
            started = []
            for pp in range(3):
                w = W.next(f"co{pp}")
                cow.append(w)
                wv = w.ap.rearrange("p (k n) -> p k n", k=8)
                for t in range(2):
                    bank = mm6()
                    for kc in range(7):
                        T.mm(bank.ap[:, :NC], wv[:, kc, t * 128:(t + 1) * 128], by_ap(kc)[:, :NC], kc == 0, False, [w.res, byr[kc]], [bank.res], signal=False)
                    started.append((bank, wv, t, w, 2 * pp + t))
            for (bank, wv, t, w, n) in started:
                T.mm(bank.ap[:, :NC], wv[:, 7, t * 128:(t + 1) * 128], by_ap(7)[:, :NC], False, True, [w.res, byr[7]], [bank.res], signal=True)
                T.tt(DVE, X[:, n, :NC], bank.ap[:, :NC], X[:, n, :NC], ALU.add, [bank.res, Xr[n]], [Xr[n]])
            W.release(3)
            w = W.next("co3")
            wv = w.ap.rearrange("p (k n) -> p k n", k=8)
            for t in range(2):
                n = 6 + t
                bank = mm6()
                proj8(bank, wv, t * 128, lambda kc: (by_ap(kc)[:, :NC], byr[kc]), NC, w.res)
                T.tt(DVE, X[:, n, :NC], bank.ap[:, :NC], X[:, n, :NC], ALU.add, [bank.res, Xr[n]], [Xr[n]])
            W.release()

        def grp_cols(gi):
            return 512 * gi, (512 if gi < 8 else 224)

        load_x(groups[0])
        l0gen = None
        for idx, gi in enumerate(groups):
            n0, NC = grp_cols(gi)
            s = gi % 2
            X, Xr = xt[s], xr[s]
            if l0gen is None:
                for _ in layer0(gi, X, Xr, NC, CF):
                    pass
            else:
                for _ in l0gen:
                    pass
                l0gen = None
            if stop != "l0":
                has_next = idx + 1 < len(groups)
                pre = (lambda: load_x(groups[idx + 1])) if (has_next and idx > 0) else None
                ffn(X, Xr, NC, "a", on_up=pre)
                if stop != "ffn0":
                    layer1(gi, X, Xr, NC)
                    if stop != "l1":
                        if has_next and idx > 0 and INTERLEAVE:
                            gn = groups[idx + 1]
                            l0gen = layer0(gn, xt[gn % 2], xr[gn % 2], grp_cols(gn)[1], CI)

                            def pull(n, g_=l0gen, cnt=[0]):
                                for _ in range(2 if cnt[0] == 0 else 1):
                                    next(g_, None)
                                cnt[0] += 1
                            ffn(X, Xr, NC, "b", on_down=pull, dring=down_int)
                        else:
                            ffn(X, Xr, NC, "b")
            if idx == 0 and len(groups) > 1:
                assert groups[0] % 2 == 0
                load_x(groups[1])
            T.dma(SP, yT[:, :, n0:n0 + NC], X[:, :, 0:NC], f"s_x{s}", reads=Xr)
        T.dma(SP, aux[:, :], aux_sb[:, :], "s_aux", reads=[auxr])
        for k_, v_ in T.dmacnt.items():
            if v_ and SP.known.get(k_, 0) < v_:
                nc.sync.wait_ge(T.sems[k_], v_)
        if order is None:
            return W.rec
        build.stats = dict(nwaits=T.nwaits, seq={e.name: e.seq for e in (PE, ACT, DVE, POOL)})
    return nc


def _rel_bucket(rel):
    half, max_exact = 16, 8
    n = np.abs(rel)
    large = max_exact + (np.log(np.maximum(n, 1).astype(np.float32) / max_exact) / np.float32(np.log(128 / max_exact)) * (half - max_exact)).astype(np.int32)
    large = np.minimum(large, half - 1)
    return np.where(rel > 0, half, 0) + np.where(n < max_exact, n, large)


def _bucket_table():
    return _rel_bucket(np.arange(-400, 401))


def _pack_weights(inp):
    f32 = np.float32
    out = np.zeros((NP_, 128, PW), f32)

    def pk(Wm, cols):
        return np.transpose(Wm.reshape(8, 128, Wm.shape[1])[:, :, cols], (1, 0, 2))
    Win = inp["ev_w_in"][0]
    tiles = []
    for j in range(4):
        tiles.append(list(range(64 * j, 64 * j + 64)) + list(range(64 * (4 + j), 64 * (4 + j) + 64)))
    tiles.append(list(range(512, 640)))
    tiles.append(list(range(640, 768)))
    for g in range(4):
        tiles.append(list(range(768 + 128 * g, 768 + 128 * (g + 1))))
    rowperm = []
    for kc in range(4):
        rowperm += list(range(64 * kc, 64 * kc + 64)) + list(range(64 * (4 + kc), 64 * (4 + kc) + 64))
    rowperm += list(range(512, 1024))
    Wout = inp["ev_w_out"][0][rowperm]
    idx = {n: i for i, (n, _) in enumerate(PIECES)}
    for i in range(5):
        out[idx[f"in{i}"]] = pk(Win, tiles[2 * i] + tiles[2 * i + 1]).reshape(128, PW)
    out[idx["poolw"], :, 0:512] = np.transpose(inp["pool_w"][0], (1, 0, 2)).reshape(128, 512)
    for i in range(4):
        out[idx[f"out{i}"]] = pk(Wout, list(range(256 * i, 256 * (i + 1)))).reshape(128, PW)
    for l, tag in enumerate("ab"):
        W1 = inp["ffn_w1"][l]
        W2 = inp["ffn_w2"][l]
        for i in range(16):
            out[idx[f"w1{tag}{i}"]] = pk(W1, list(range(256 * i, 256 * (i + 1)))).reshape(128, PW)
        W2r = W2.reshape(32, 128, 1024)
        for n in range(8):
            for hh in range(2):
                blk_ = W2r[16 * hh:16 * (hh + 1), :, 128 * n:128 * (n + 1)]
                out[idx[f"w2{tag}{2 * n + hh}"]] = np.transpose(blk_, (1, 0, 2)).reshape(128, PW)
    Wc = inp["conv_w_in"][0]
    for f in range(8):
        out[idx[f"ca{f}"]] = pk(Wc, list(range(128 * f, 128 * (f + 1))) + list(range(1024 + 128 * f, 1024 + 128 * (f + 1)))).reshape(128, PW)
    for h in range(4):
        out[idx[f"ch{h}"]] = pk(Wc, list(range(2048 + 256 * h, 2048 + 256 * (h + 1)))).reshape(128, PW)
    Wco = inp["conv_w_out"][0]
    for i in range(4):
        out[idx[f"co{i}"]] = pk(Wco, list(range(256 * i, 256 * (i + 1)))).reshape(128, PW)
    return out


def _pack_consts(inp, core):
    f32 = np.float32
    c = np.zeros((128, NCST), f32)
    first = (core % 4 == 0)
    for base, v in ((C_MIX0, inp["norm_mix"][0]), (C_FFN0, inp["norm_ffn"][0]), (C_MIX1, inp["norm_mix"][1]), (C_FFN1, inp["norm_ffn"][1])):
        c[:, base:base + 8] = v.reshape(8, 128).T
    c[:, C_PSC:C_PSC + 4] = inp["pool_scale"][0].reshape(4, 128).T
    c[:, C_PSC + 4] = 1.0
    c[:, C_GQ] = np.tile(inp["q_norm"][0], 2)
    c[:, C_GK] = np.tile(inp["k_norm"][0], 2)
    for j in range(3):
        c[:, C_CW + 8 * j:C_CW + 8 * j + 8] = inp["conv_w"][0][j].reshape(8, 128).T
    sk = inp["attn_sinks"][0]
    c[0:64, C_SINK:C_SINK + 4] = sk[0:4][None, :]
    c[64:128, C_SINK:C_SINK + 4] = sk[4:8][None, :]
    if first:
        c[:, C_NMA] = NEG
        c[0:64, C_NMH] = NEG
    for g in range(4):
        w = 2 ** (g + 1)
        for j in range(16):
            c[:, C_INVC + 16 * g + j] = 1.0 / (min(j + 1, w) if first else w)
    c[:, C_ID:C_ID + 128] = np.eye(128, dtype=f32)
    return c


def _pack_bias(inp):
    f32 = np.float32
    rb = inp["rel_bias"]
    bt = _bucket_table()

    def bk(rel):
        return bt[rel + 400]
    out = np.zeros((128, NBIA), f32)
    k = np.arange(128)[:, None]
    q = np.arange(64)[None, :]
    b0 = rb[bk(k - 128 - q)]
    k64 = (np.arange(128) % 64)[:, None]
    b1 = rb[bk(k64 - q)]

    def lay(b, nq):
        return np.transpose(b.reshape(128, nq, 2, 4), (0, 2, 3, 1)).reshape(128, 8 * nq)
    out[:, 0:512] = lay(b0, 64)
    out[:, 512:1024] = lay(b1, 64)
    qs = np.arange(16)[None, :]
    bs0 = rb[bk((896 + k) - (1024 + qs))]
    k16 = np.minimum(np.arange(128), 15)[:, None]
    bs1 = rb[bk(k16 - qs)]
    out[:, 1024:1152] = lay(bs0, 16)
    out[:, 1152:1280] = lay(bs1, 16)
    return out


_NC_CACHE = {}


def _get_nc():
    if "nc" not in _NC_CACHE:
        _NC_CACHE["nc"] = build()
    return _NC_CACHE["nc"]


def make_in_maps(inp):
    f32 = np.float32
    inp = {k: np.asarray(v) for k, v in inp.items()}
    w32 = _pack_weights(inp)
    bia = _pack_bias(inp)
    maps = []
    for core in range(NCORE):
        b, qd = core // 4, core % 4
        s0 = qd * TOK
        rows = np.zeros((NCOL, 1024), f32)
        lo = s0 - HALO
        if lo >= 0:
            rows[0:HALO + TOK] = inp["x_prompt"][b, lo:s0 + TOK]
        else:
            rows[HALO:HALO + TOK] = inp["x_prompt"][b, s0:s0 + TOK]
        rows[HALO + TOK + 16:] = inp["x_sample"][core]
        xin = np.ascontiguousarray(np.transpose(rows.reshape(NCOL, 8, 128), (2, 1, 0)))
        cach = np.zeros((128, 3 * 128 + 64 + 16), f32)
        ck = inp["cache_k"][0, core].reshape(128, 128)
        cv = inp["cache_v"][0, core].reshape(128, 128)
        cach[:, 0:128] = ck.T
        cach[:, 128:256] = cv.T
        cach[:, 256:384] = cv
        sp = inp["state_pool"][0, core]
        spt = np.zeros((128, 4, 16), f32)
        spt[:, :, 1:16] = np.transpose(sp.reshape(15, 4, 128), (2, 1, 0))
        cach[:, 384:448] = spt.reshape(128, 64)
        sc = inp["state_conv"][0, core]
        cach[:, 448:464] = np.transpose(sc.reshape(2, 8, 128), (2, 1, 0)).reshape(128, 16)
        maps.append({"xin": xin, "w32": w32, "cst": _pack_consts(inp, core), "bia": bia, "cach": cach})
    return maps


def assemble(results):
    f32 = np.float32
    y_p = np.zeros((2, 16384, 1024), f32)
    y_s = np.zeros((8, 16, 1024), f32)
    k_p = np.zeros((1, 2, 128, 2, 64), f32); v_p = np.zeros_like(k_p)
    pool_p = np.zeros((1, 2, 15, 512), f32); conv_p = np.zeros((1, 2, 2, 1024), f32)
    k_s = np.zeros((1, 8, 128, 2, 64), f32); v_s = np.zeros_like(k_s)
    pool_s = np.zeros((1, 8, 15, 512), f32); conv_s = np.zeros((1, 8, 2, 1024), f32)
    for core in range(NCORE):
        r = results[core]
        yT = np.asarray(r["yT"]); ax = np.asarray(r["aux"])
        b, qd = core // 4, core % 4
        y_p[b, qd * TOK:(qd + 1) * TOK] = np.transpose(yT[:, :, HALO:HALO + TOK], (2, 1, 0)).reshape(TOK, 1024)
        y_s[core] = np.transpose(yT[:, :, HALO + TOK + 16:], (2, 1, 0)).reshape(16, 1024)
        if qd == 3:
            k_p[0, b] = ax[:, A_KP:A_KP + 128].T.reshape(128, 2, 64)
            v_p[0, b] = ax[:, A_VP:A_VP + 128].T.reshape(128, 2, 64)
            pool_p[0, b] = np.transpose(ax[:, A_PP:A_PP + 60].reshape(128, 4, 15), (2, 1, 0)).reshape(15, 512)
            conv_p[0, b] = np.transpose(ax[:, A_CP:A_CP + 16].reshape(128, 8, 2), (2, 1, 0)).reshape(2, 1024)
        k_s[0, core] = ax[:, A_KS:A_KS + 128].T.reshape(128, 2, 64)
        v_s[0, core] = ax[:, A_VS:A_VS + 128].T.reshape(128, 2, 64)
        pool_s[0, core] = np.transpose(ax[:, A_PS:A_PS + 60].reshape(128, 4, 15), (2, 1, 0)).reshape(15, 512)
        conv_s[0, core] = np.transpose(ax[:, A_CS:A_CS + 16].reshape(128, 8, 2), (2, 1, 0)).reshape(2, 1024)
    return (y_p, y_s, k_p, v_p, pool_p, conv_p, k_s, v_s, pool_s, conv_s)


def kernel(**inputs):
    nc = _get_nc()
    maps = make_in_maps(inputs)
    res = run_bass_kernel_spmd(nc, maps, core_ids=list(range(NCORE)))
    return assemble(res.results)
```

```python
import numpy as np
from contextlib import ExitStack
import concourse.bass as bass
import concourse.mybir as mybir
from concourse.bass_utils import run_bass_kernel_spmd

F32 = mybir.dt.float32
BF16 = mybir.dt.bfloat16
AF = mybir.ActivationFunctionType
ALU = mybir.AluOpType

NCORE = 8
TOK = 4096
HALO = 192
NCOL = HALO + TOK + 32
NGRP = 9
EPS = 1e-6
NEG = -1.0e4
PW = 2048
NW = 8
INTERLEAVE = True

C_MIX0, C_FFN0, C_MIX1, C_FFN1, C_PSC, C_GQ, C_GK, C_CW, C_SINK, C_NMA, C_NMH, C_INVC, C_ID = 0, 8, 16, 24, 32, 37, 38, 39, 63, 67, 68, 69, 133
NCST = 261
A_KP, A_VP, A_PP, A_CP, A_KS, A_VS, A_PS, A_CS = 0, 128, 256, 316, 332, 460, 588, 648
AUXW = 664
NBIA = 2 * 512 + 256


def piece_table():
    P = []
    def kc8(base):
        return [(0, 2048, None)]
    for i in range(5):
        P.append((f"in{i}", kc8(C_MIX0)))
    P.append(("poolw", [(0, 2048, None)]))
    for i in range(4):
        P.append((f"out{i}", [(0, 2048, None)]))
    for i in range(16):
        P.append((f"w1a{i}", kc8(C_FFN0)))
    for i in range(16):
        P.append((f"w2a{i}", [(0, 2048, None)]))
    for f in range(8):
        P.append((f"ca{f}", kc8(C_MIX1)))
        if f % 2 == 0:
            P.append((f"ch{f // 2}", kc8(C_MIX1)))
    for i in range(4):
        P.append((f"co{i}", [(0, 2048, None)]))
    for i in range(16):
        P.append((f"w1b{i}", kc8(C_FFN1)))
    for i in range(16):
        P.append((f"w2b{i}", [(0, 2048, None)]))
    return P


PIECES = piece_table()
NP_ = len(PIECES)
PIDX = {n: i for i, (n, _) in enumerate(PIECES)}


class Res:
    __slots__ = ("name", "w", "r")

    def __init__(self, name):
        self.name = name
        self.w = None
        self.r = {}


class Eng:
    def __init__(self, name, h, key):
        self.name = name
        self.h = h
        self.key = key
        self.seq = 0
        self.known = {}


class Slot:
    __slots__ = ("ap", "res", "t")

    def __init__(self, t, res):
        self.t = t
        self.ap = t
        self.res = res


class Tracker:
    def __init__(self, nc, es):
        self.nc = nc
        self.es = es
        self.sems = {}
        self.snap = {}
        self.dmacnt = {}
        self.PE = self._eng("pe", nc.tensor)
        self.ACT = self._eng("act", nc.scalar)
        self.DVE = self._eng("dve", nc.vector)
        self.POOL = self._eng("pool", nc.gpsimd)
        self.SP = Eng("sp", nc.sync, "q_sp")
        self.nwaits = 0

    def _eng(self, name, h):
        key = "e_" + name
        self.sems[key] = self.es.enter_context(self.nc.semaphore(key))
        return Eng(name, h, key)

    def dsem(self, key):
        if key not in self.sems:
            self.sems[key] = self.es.enter_context(self.nc.semaphore(key))
            self.dmacnt[key] = 0
        return key

    def _deps(self, E, reads, writes):
        raw = {}
        war = {}

        def add(d, tgt):
            if d is None:
                return
            k, v = d
            if tgt.get(k, 0) < v:
                tgt[k] = v
        for r in reads:
            add(r.w, raw)
        for w in writes:
            add(w.w, raw)
            for k, v in w.r.items():
                add((k, v), war)
        deps = dict(raw)
        for k, v in war.items():
            if k == E.key and E is not self.POOL:
                continue
            if deps.get(k, 0) < v:
                deps[k] = v
        if E is self.PE and E.key in deps:
            del deps[E.key]
        for k, v in deps.items():
            if E.known.get(k, 0) >= v:
                continue
            E.h.wait_ge(self.sems[k], v)
            self.nwaits += 1
            E.known[k] = v
            sn = self.snap.get((k, v))
            if sn:
                for k2, v2 in sn.items():
                    if E.known.get(k2, 0) < v2:
                        E.known[k2] = v2

    def op(self, E, fn, reads=(), writes=(), signal=True):
        self._deps(E, reads, writes)
        ins = fn()
        stamp = E.seq + 1
        if signal:
            E.seq = stamp
            ins.then_inc(self.sems[E.key], 1)
            sn = dict(E.known)
            self.snap[(E.key, stamp)] = sn
        for r in reads:
            if r.r.get(E.key, 0) < stamp:
                r.r[E.key] = stamp
        for w in writes:
            w.w = (E.key, stamp)
            w.r = {}
        return ins

    def dma(self, Q, out, in_, semkey, reads=(), writes=()):
        self.dsem(semkey)
        self._deps(Q, reads, writes)
        ins = Q.h.dma_start(out=out, in_=in_)
        self.dmacnt[semkey] += 16
        val = self.dmacnt[semkey]
        ins.then_inc(self.sems[semkey], 16)
        self.snap[(semkey, val)] = dict(Q.known)
        for r in reads:
            if r.r.get(semkey, 0) < val:
                r.r[semkey] = val
        for w in writes:
            w.w = (semkey, val)
            w.r = {}

    def barrier(self):
        engs = [self.PE, self.ACT, self.DVE, self.POOL]
        for E in engs + [self.SP]:
            for O in engs:
                if O is E or O.seq == 0:
                    continue
                if E.known.get(O.key, 0) < O.seq:
                    E.h.wait_ge(self.sems[O.key], O.seq)
                    E.known[O.key] = O.seq
            for k, v in self.dmacnt.items():
                if v and E.known.get(k, 0) < v:
                    E.h.wait_ge(self.sems[k], v)
                    E.known[k] = v

    def act(self, out, in_, func, reads, writes, scale=None, bias=None):
        kw = {}
        if scale is not None:
            kw["scale"] = scale
        if bias is not None:
            kw["bias"] = bias
        return self.op(self.ACT, lambda: self.nc.scalar.activation(out=out, in_=in_, func=func, **kw), reads, writes)

    def tt(self, E, out, in0, in1, op, reads, writes):
        return self.op(E, lambda: E.h.tensor_tensor(out=out, in0=in0, in1=in1, op=op), reads, writes)

    def ts(self, E, out, in0, s1, op0, reads, writes, s2=None, op1=None):
        if op1 is None:
            return self.op(E, lambda: E.h.tensor_scalar(out=out, in0=in0, scalar1=s1, scalar2=None, op0=op0), reads, writes)
        return self.op(E, lambda: E.h.tensor_scalar(out=out, in0=in0, scalar1=s1, scalar2=s2, op0=op0, op1=op1), reads, writes)

    def stt(self, out, in0, scalar, in1, op0, op1, reads, writes):
        return self.op(self.DVE, lambda: self.nc.vector.scalar_tensor_tensor(out=out, in0=in0, scalar=scalar, in1=in1, op0=op0, op1=op1), reads, writes)

    def copy(self, E, out, in_, reads, writes):
        if E is self.ACT:
            return self.act(out, in_, AF.Copy, reads, writes)
        return self.op(E, lambda: E.h.tensor_copy(out=out, in_=in_), reads, writes)

    def memset(self, E, ap, val, writes):
        return self.op(E, lambda: E.h.memset(ap, val), (), writes)

    def mm(self, out, lhsT, rhs, start, stop, reads, writes, signal=True):
        return self.op(self.PE, lambda: self.nc.tensor.matmul(out, lhsT=lhsT, rhs=rhs, start=start, stop=stop), reads, writes, signal)

    def tr(self, out, in_, ident, reads, writes):
        return self.op(self.PE, lambda: self.nc.tensor.transpose(out, in_, ident), reads, writes)


class Ring:
    def __init__(self, slots):
        self.slots = slots
        self.i = 0

    def __call__(self):
        s = self.slots[self.i % len(self.slots)]
        self.i += 1
        return s


def build(groups=None, stop=None):
    order = _build(groups, stop, None)
    return _build(groups, stop, order)


def _build(groups, stop, order):
    if groups is None:
        groups = list(range(NGRP))
    nc = bass.Bass("TRN2", target_bir_lowering=False)
    xin = nc.dram_tensor("xin", [128, 8, NCOL], F32, kind="ExternalInput").ap()
    w32 = nc.dram_tensor("w32", [NP_, 128, PW], F32, kind="ExternalInput").ap()
    cst = nc.dram_tensor("cst", [128, NCST], F32, kind="ExternalInput").ap()
    bia = nc.dram_tensor("bia", [128, NBIA], F32, kind="ExternalInput").ap()
    cach = nc.dram_tensor("cach", [128, 3 * 128 + 64 + 16], F32, kind="ExternalInput").ap()
    yT = nc.dram_tensor("yT", [128, 8, NCOL], F32, kind="ExternalOutput").ap()
    aux = nc.dram_tensor("aux", [128, AUXW], F32, kind="ExternalOutput").ap()
    wbf = nc.dram_tensor("wbf", [NP_, 128, PW], BF16, kind="Internal").ap()

    with ExitStack() as es:
        T = Tracker(nc, es)
        PE, ACT, DVE, POOL, SP = T.PE, T.ACT, T.DVE, T.POOL, T.SP

        def sb(name, shape, dt):
            return es.enter_context(nc.sbuf_tensor(name, shape, dt))

        def ps(name, shape, dt):
            return es.enter_context(nc.psum_tensor(name, shape, dt))

        xt = [sb(f"x{i}", [128, 8, 512], F32) for i in range(2)]
        xr = [[Res(f"x{i}_{k}") for k in range(8)] for i in range(2)]
        xb = sb("xb", [128, 8, 512], BF16)
        xbr = [Res(f"xb{k}") for k in range(8)]
        rbc = sb("rbc", [128, 512], F32); rbcr = Res("rbc")
        rr = sb("rr", [128, 512], F32); rrr = Res("rr")
        qn = sb("qn", [128, 4, 512], BF16); qnr = [Res(f"qn{j}") for j in range(4)]
        kT = [sb(f"kT{g}", [128, 768], BF16) for g in range(2)]
        kTh = Res("kTh"); kTc = Res("kTc")
        vT = sb("vT", [128, 64 + 512], BF16); vTh = Res("vTh"); vTc = Res("vTc")
        VA = [sb(f"VA{g}", [128, 6, 128], BF16) for g in range(2)]; VAr = [Res(f"VA{s}") for s in range(6)]
        VB = [sb(f"VB{g}", [128, 4, 128], BF16) for g in range(2)]; VBr = [Res(f"VB{s}") for s in range(4)]
        Vs = [sb(f"Vs{g}", [128, 128], BF16) for g in range(2)]; Vsr = Res("Vs")
        u = sb("u", [128, 4, 528], F32); uh = [Res(f"uh{g}") for g in range(4)]; uc = [Res(f"uc{g}") for g in range(4)]
        dp = sb("dp", [128, 4, 512], BF16); dpr = [Res(f"dp{g}") for g in range(4)]
        pm = sb("pm", [128, 4, 512], BF16); pmr = [Res(f"pm{g}") for g in range(4)]
        byr = dpr + pmr
        by_ap = lambda f: dp[:, f, :] if f < 4 else pm[:, f - 4, :]
        stg = [sb(f"stg{i}", [128, PW], F32) for i in range(2)]; stgr = [Res(f"stg{i}") for i in range(2)]
        aT = sb("aT", [128, 4, 512], BF16); aTr = Res("aT")
        pT1 = [sb("pT1t", [128, 512], BF16), sb("pT1b", [128, 512], BF16)]; pT1r = [Res("pT1t"), Res("pT1b")]
        pT1s = sb("pT1s", [128, 128], BF16); pT1sr = Res("pT1s")
        hid = sb("hid", [128, 32, 512], BF16); hidr = [Res(f"hid{k}") for k in range(32)]
        chh = sb("chh", [128, 8, 2], F32); chhr = [Res(f"chh{k}") for k in range(8)]
        wring = [Slot(sb(f"w{i}", [128, PW], BF16), Res(f"w{i}")) for i in range(NW)]
        fscr = Ring([Slot(sb(f"fs{i}", [128, 528], F32), Res(f"fs{i}")) for i in range(9)])
        pscr = Ring([Slot(sb(f"pscr{i}", [128, 528], F32), Res(f"pscr{i}")) for i in range(4)])
        bscr = Ring([Slot(sb(f"bs{i}", [128, 512], BF16), Res(f"bs{i}")) for i in range(4)])
        cst_sb = sb("cst_sb", [128, NCST], F32); cstr = Res("cst")
        bia_sb = sb("bia_sb", [128, NBIA], F32); biar = Res("bia")
        cach_sb = sb("cach_sb", [128, 3 * 128 + 64 + 16], F32); cachr = Res("cach")
        ckm = [sb(f"ckm{g}", [128, 128], BF16) for g in range(2)]
        cvm = [sb(f"cvm{g}", [128, 128], BF16) for g in range(2)]
        ckvr = Res("ckv")
        aux_sb = sb("aux_sb", [128, AUXW], F32); auxr = Res("aux")
        onesb = sb("onesb", [128, 128], BF16)
        blk = sb("blk", [128, 128], BF16)
        hones = [sb(f"hones{g}", [128, 128], BF16) for g in range(2)]
        identb = sb("identb", [128, 128], BF16)
        esink = sb("esink", [128, 4], F32)
        konst = Res("konst")
        banks = [Slot(ps(f"pb{i}", [128, 512], F32), Res(f"pb{i}")) for i in range(7)]
        tp = ps("tp", [128, 1024], BF16); tpr = [Res(f"tp{i}") for i in range(8)]
        misc = banks[3]

        class Cfg:
            pass
        CF = Cfg()
        CF.mm = Ring([banks[0], banks[1], banks[2]]); CF.misc = banks[3]
        CF.st0 = Ring([banks[4], banks[0]]); CF.st1 = Ring([banks[5], banks[1]]); CF.od = Ring([banks[6], banks[3]])
        CF.qk = [banks[0], banks[1], banks[2], banks[4], banks[5]]
        CI = Cfg()
        CI.mm = Ring([banks[0], banks[1]]); CI.misc = banks[3]
        CI.st0 = Ring([banks[4]]); CI.st1 = Ring([banks[5]]); CI.od = Ring([banks[3]])
        CI.qk = [banks[0], banks[1], banks[4], banks[5], banks[3]]
        down_int = Ring([banks[2], banks[6]])
        st0, st1 = banks[4], banks[5]
        mm3 = Ring([banks[0], banks[1], banks[2]])
        mm6 = Ring([banks[0], banks[1], banks[2], banks[4], banks[5], banks[6]])

        T.dma(SP, cst_sb[:], cst[:, :], "s_setup", writes=[cstr])
        T.dma(SP, bia_sb[:], bia[:, :], "s_setup", writes=[biar])
        T.dma(SP, cach_sb[:], cach[:, :], "s_setup", writes=[cachr])
        tot = T.dmacnt["s_setup"]
        for r_ in (cstr, biar, cachr):
            r_.w = ("s_setup", tot)
        T.memset(DVE, onesb[:], 1.0, [konst])
        T.memset(DVE, blk[:], 0.0, [konst])
        T.memset(DVE, blk[0:64, 0:64], 1.0, [konst])
        T.memset(DVE, blk[64:128, 64:128], 1.0, [konst])
        for g in range(2):
            T.memset(DVE, hones[g][:], 0.0, [konst])
            T.memset(DVE, hones[g][:, g * 64:(g + 1) * 64], 1.0, [konst])
        T.copy(DVE, identb[:], cst_sb[:, C_ID:C_ID + 128], [cstr], [konst])
        T.act(esink[:], cst_sb[:, C_SINK:C_SINK + 4], AF.Exp, [cstr], [konst])
        for g in range(2):
            T.memset(POOL, kT[g][:], 0.0, [kTh, kTc])
            T.memset(POOL, VA[g][:], 0.0, VAr)
            T.memset(POOL, VB[g][:], 0.0, VBr)
            T.memset(POOL, Vs[g][:], 0.0, [Vsr])
            T.memset(POOL, ckm[g][:], 0.0, [ckvr])
            T.memset(POOL, cvm[g][:], 0.0, [ckvr])
            T.memset(POOL, pT1[g][:], 0.0, [pT1r[g]])
        T.memset(POOL, pT1s[:], 0.0, [pT1sr])
        T.memset(POOL, vT[:], 0.0, [vTh, vTc])
        T.memset(POOL, u[:], 0.0, uh + uc)
        T.memset(POOL, chh[:], 0.0, chhr)
        T.memset(POOL, aT[:], 0.0, [aTr])
        T.memset(POOL, aux_sb[:], 0.0, [auxr])
        CK, CVT, CV, SPT, SCT = 0, 128, 256, 384, 448
        for g in range(2):
            T.copy(DVE, ckm[g][g * 64:(g + 1) * 64, :], cach_sb[g * 64:(g + 1) * 64, CK:CK + 128], [cachr], [ckvr])
            T.copy(DVE, cvm[g][:, g * 64:(g + 1) * 64], cach_sb[:, CV + g * 64:CV + (g + 1) * 64], [cachr], [ckvr])
        T.copy(DVE, aux_sb[:, A_KS:A_KS + 112], cach_sb[:, CK + 16:CK + 128], [cachr], [auxr])
        T.copy(DVE, aux_sb[:, A_VS:A_VS + 112], cach_sb[:, CVT + 16:CVT + 128], [cachr], [auxr])

        wbfr = [Res(f"wbf{p}") for p in range(NP_)]

        NSTG = 4

        def stage_ap(i):
            if i < 2:
                return xt[1][:, 4 * i:4 * i + 4, :].rearrange("p a b -> p (a b)")
            return stg[i - 2][:, :]

        def stage_res(i):
            return xr[1][4 * i:4 * i + 4] if i < 2 else [stgr[i - 2]]

        class WS:
            def __init__(self):
                self.issued = 0
                self.loaded = 0
                self.taken = 0
                self.done = 0
                self.cnt = 0
                self.nconv = 0
                self.rec = [] if order is None else None
                self.order = order or []
                self.total = len(self.order)
                self.first = {}
                for i_, p_ in enumerate(self.order):
                    self.first.setdefault(p_, i_)
                assert order is None or sorted(self.first, key=self.first.get) == list(range(NP_))

            def _convert(self, p, slot):
                sf = stage_ap(p % NSTG)
                sres = stage_res(p % NSTG)
                wsl = wring[slot]
                for (a_, ln_, col) in PIECES[p][1]:
                    E = DVE if self.cnt % 2 == 0 else ACT
                    self.cnt += 1
                    if col is None:
                        T.copy(E, wsl.ap[:, a_:a_ + ln_], sf[:, a_:a_ + ln_], sres, [wsl.res])
                    elif E is DVE:
                        T.ts(DVE, wsl.ap[:, a_:a_ + ln_], sf[:, a_:a_ + ln_], cst_sb[:, col:col + 1], ALU.mult, sres + [cstr], [wsl.res])
                    else:
                        T.act(wsl.ap[:, a_:a_ + ln_], sf[:, a_:a_ + ln_], AF.Copy, sres + [cstr], [wsl.res], scale=cst_sb[:, col:col + 1])
                T.dma(SP, wbf[p], wsl.ap[:, :], f"s_wo{slot}", reads=[wsl.res], writes=[wbfr[p]])

            def _stage_loads(self):
                while self.loaded < NP_ and self.loaded < self.nconv + NSTG:
                    p = self.loaded
                    T.dma(SP, stage_ap(p % NSTG), w32[p], f"s_stg{p % NSTG}", writes=stage_res(p % NSTG))
                    self.loaded += 1

            def _pump(self):
                self._stage_loads()
                while self.issued < self.total and self.issued < self.done + NW:
                    s_ = self.issued
                    p_ = self.order[s_]
                    if self.first[p_] == s_:
                        assert p_ == self.nconv
                        self._convert(p_, s_ % NW)
                        self.nconv += 1
                        self.issued += 1
                        self._stage_loads()
                    else:
                        sl = wring[s_ % NW]
                        T.dma(SP, sl.ap[:, :], wbf[p_], f"s_w{s_ % NW}", reads=[wbfr[p_]], writes=[sl.res])
                        self.issued += 1

            def next(self, name):
                if self.rec is not None:
                    self.rec.append(PIDX[name])
                    self.taken += 1
                    return wring[(self.taken - 1) % NW]
                self._pump()
                s_ = self.taken
                assert PIECES[self.order[s_]][0] == name, (PIECES[self.order[s_]][0], name)
                assert s_ < self.issued
                self.taken += 1
                return wring[s_ % NW]

            def release(self, n=1):
                if self.rec is not None:
                    return
                self.done += n
                self._pump()
        W = WS()

        def load_x(gi):
            n0 = 512 * gi
            NC = 512 if gi < 8 else 224
            s = gi % 2
            T.dma(SP, xt[s][:, :, 0:NC], xin[:, :, n0:n0 + NC], f"s_x{s}", writes=xr[s])

        def norm(X, Xr, NC, want_r, want_rr, misc=misc, gbase=None):
            for kc in range(8):
                sq = bscr()
                T.act(sq.ap[:, :NC], X[:, kc, :NC], AF.Square, [Xr[kc]], [sq.res])
                T.ts(DVE, xb[:, kc, :NC], X[:, kc, :NC], cst_sb[:, gbase + kc:gbase + kc + 1], ALU.mult, [Xr[kc], cstr], [xbr[kc]])
                T.mm(misc.ap[:, :NC], onesb[:, :], sq.ap[:, :NC], kc == 0, kc == 7, [sq.res, konst], [misc.res])
            ln = fscr()
            T.act(ln.ap[:, :NC], misc.ap[:, :NC], AF.Ln, [misc.res], [ln.res], scale=1.0 / 1024, bias=EPS)
            if want_r:
                T.act(rbc[:, :NC], ln.ap[:, :NC], AF.Exp, [ln.res], [rbcr], scale=-0.5)
            if want_rr:
                T.act(rr[:, :NC], ln.ap[:, :NC], AF.Exp, [ln.res], [rrr], scale=-1.0)

        def proj8(bank, wv, c0, rhs_fn, NC, wres):
            for kc in range(8):
                rap, rres = rhs_fn(kc)
                T.mm(bank.ap[:, :NC], wv[:, kc, c0:c0 + 128], rap, kc == 0, kc == 7, [wres, rres], [bank.res], signal=(kc == 7))

        def ffn(X, Xr, NC, tag, on_up=None, on_down=None, dring=None):
            norm(X, Xr, NC, False, True, gbase=(C_FFN0 if tag == "a" else C_FFN1))
            for pp in range(16):
                w = W.next(f"w1{tag}{pp}")
                wv = w.ap.rearrange("p (k n) -> p k n", k=8)
                for t in range(2):
                    ht = 2 * pp + t
                    bank = mm6()
                    proj8(bank, wv, t * 128, lambda kc: (xb[:, kc, :NC], xbr[kc]), NC, w.res)
                    a = fscr()
                    T.act(a.ap[:, :NC], bank.ap[:, :NC], AF.Relu, [bank.res], [a.res])
                    T.stt(hid[:, ht, :NC], bank.ap[:, :NC], 0.0, a.ap[:, :NC], ALU.max, ALU.mult, [bank.res, a.res], [hidr[ht]])
                W.release()
                if pp == 7 and on_up is not None:
                    on_up()
            for n in range(8):
                bank = mm6() if dring is None else dring()
                for hh in range(2):
                    w = W.next(f"w2{tag}{2 * n + hh}")
                    wv = w.ap.rearrange("p (k n) -> p k n", k=16)
                    for kl in range(16):
                        kc = 16 * hh + kl
                        T.mm(bank.ap[:, :NC], wv[:, kl, :], hid[:, kc, :NC], kc == 0, kc == 31, [w.res, hidr[kc]], [bank.res], signal=(kl == 15))
                    if hh == 0 and on_down is not None:
                        W.release(1)
                        on_down(n)
                t_ = fscr()
                T.tt(DVE, t_.ap[:, :NC], bank.ap[:, :NC], rr[:, :NC], ALU.mult, [bank.res, rrr], [t_.res])
                addE = DVE if (on_down is not None or n % 2 == 1) else POOL
                T.tt(addE, X[:, n, :NC], X[:, n, :NC], t_.ap[:, :NC], ALU.add, [Xr[n], t_.res], [Xr[n]])
                W.release(1 if on_down is not None else 2)
                if on_down is not None:
                    on_down(n)

        def attn_scores(C, qc0, nq, k0, k0res, v0, v0res, b0, b0res, lo, hi, k1, v1, v1res, b1, pbuf, pres):
            Wd = 4 * nq
            st0, st1 = C.st0(), C.st1()
            qres = qnr + [kTh, kTc]
            q3 = qn[:, 0:4, qc0:qc0 + nq]
            for g in range(2):
                T.mm(st0.ap[:, g * Wd:(g + 1) * Wd].rearrange("p (j q) -> p j q", j=4), k0(g), q3, True, True, qres + k0res, [st0.res], signal=(g == 1))
            for g in range(2):
                T.mm(st1.ap[:, g * Wd:(g + 1) * Wd].rearrange("p (j q) -> p j q", j=4), k1(g), q3, True, True, qres, [st1.res], signal=(g == 1))
            s0 = fscr()
            T.stt(s0.ap[:, :2 * Wd], st0.ap[:, :2 * Wd], 0.125, b0, ALU.mult, ALU.add, [st0.res, b0res], [s0.res])
            p0 = bscr()
            T.act(p0.ap[:, :2 * Wd], s0.ap[:, :2 * Wd], AF.Exp, [s0.res], [p0.res])
            s1 = fscr()
            T.stt(s1.ap[lo:hi, :2 * Wd], st1.ap[lo:hi, :2 * Wd], 0.125, b1[lo:hi, :], ALU.mult, ALU.add, [st1.res, biar], [s1.res])
            T.act(pbuf[lo:hi, :2 * Wd], s1.ap[lo:hi, :2 * Wd], AF.Exp, [s1.res], [pres])
            return (C, qc0, nq, v0, v0res, v1, v1res, pbuf, pres, p0)

        def attn_out(state):
            C, qc0, nq, v0, v0res, v1, v1res, pbuf, pres, p0 = state
            Wd = 4 * nq
            od = C.od()
            mms = [(v0(g), p0.ap[:, g * Wd:(g + 1) * Wd], [p0.res] + v0res) for g in range(2)] + \
                  [(v1(g), pbuf[:, g * Wd:(g + 1) * Wd], [pres] + v1res) for g in range(2)]
            for i, (l_, r_, rs) in enumerate(mms):
                T.mm(od.ap[:, 0:Wd], l_, r_, i == 0, i == 3, rs, [od.res], signal=False)
            dms = [(hones[g][:, :], p0.ap[:, g * Wd:(g + 1) * Wd], [p0.res, konst]) for g in range(2)] + \
                  [(hones[g][:, :], pbuf[:, g * Wd:(g + 1) * Wd], [pres, konst]) for g in range(2)]
            for i, (l_, r_, rs) in enumerate(dms):
                T.mm(od.ap[:, 256:256 + Wd], l_, r_, i == 0, i == 3, rs, [od.res], signal=(i == 3))
            ds = fscr()
            d3 = ds.ap[:, 0:Wd].rearrange("p (j q) -> p j q", j=4)
            T.tt(DVE, d3, od.ap[:, 256:256 + Wd].rearrange("p (j q) -> p j q", j=4),
                 esink[:, 0:4].unsqueeze(2).broadcast_to([128, 4, nq]), ALU.add, [od.res, konst], [ds.res])
            T.act(ds.ap[:, 0:Wd], ds.ap[:, 0:Wd], AF.Ln, [ds.res], [ds.res])
            T.act(ds.ap[:, 0:Wd], ds.ap[:, 0:Wd], AF.Exp, [ds.res], [ds.res], scale=-1.0)
            T.tt(DVE, aT[:, 0:4, qc0:qc0 + nq], od.ap[:, 0:Wd].rearrange("p (j q) -> p j q", j=4), d3, ALU.mult, [od.res, ds.res], [aTr])

        def layer0(gi, X, Xr, NC, C):
            last = gi == 8
            norm(X, Xr, NC, True, False, C.misc, gbase=C_MIX0)
            yield
            xrhs = lambda kc: (xb[:, kc, :NC], xbr[kc])
            qfs = []
            for half in range(2):
                w = W.next(f"in{half}")
                wv = w.ap.rearrange("p (k n) -> p k n", k=8)
                for t in range(2):
                    bank = C.mm()
                    proj8(bank, wv, t * 128, xrhs, NC, w.res)
                    qf = fscr()
                    T.tt(DVE, qf.ap[:, :NC], bank.ap[:, :NC], rbc[:, :NC], ALU.mult, [bank.res, rbcr], [qf.res])
                    qfs.append(qf)
                W.release()
                yield
            w = W.next("in2")
            wv = w.ap.rearrange("p (k n) -> p k n", k=8)
            bank = C.mm()
            proj8(bank, wv, 0, xrhs, NC, w.res)
            kf = fscr()
            T.tt(DVE, kf.ap[:, :NC], bank.ap[:, :NC], rbc[:, :NC], ALU.mult, [bank.res, rbcr], [kf.res])
            bank = C.mm()
            proj8(bank, wv, 128, xrhs, NC, w.res)
            T.tt(DVE, vT[:, 64:64 + NC], bank.ap[:, :NC], rbc[:, :NC], ALU.mult, [bank.res, rbcr], [vTc])
            if last:
                T.tt(DVE, aux_sb[:, A_VP:A_VP + 128], bank.ap[:, 64:192], rbc[:, 64:192], ALU.mult, [bank.res, rbcr], [auxr])
                T.tt(DVE, aux_sb[:, A_VS + 112:A_VS + 128], bank.ap[:, 208:224], rbc[:, 208:224], ALU.mult, [bank.res, rbcr], [auxr])
            W.release()
            yield
            for half in range(2):
                w = W.next(f"in{3 + half}")
                wv = w.ap.rearrange("p (k n) -> p k n", k=8)
                for t in range(2):
                    gq_ = 2 * half + t
                    bank = C.mm()
                    proj8(bank, wv, t * 128, xrhs, NC, w.res)
                    T.tt(DVE, u[:, gq_, 16:16 + NC], bank.ap[:, :NC], rbc[:, :NC], ALU.mult, [bank.res, rbcr], [uc[gq_]])
                    if last:
                        T.copy(DVE, u[:, gq_, 16 + 192:16 + 208], cach_sb[:, SPT + 16 * gq_:SPT + 16 * gq_ + 16], [cachr], [uc[gq_]])
                W.release()
                yield
            def qk_sq(qf):
                sq = bscr()
                T.act(sq.ap[:, :NC], qf.ap[:, :NC], AF.Square, [qf.res], [sq.res])
                return sq

            def qk_mm(sq, qb):
                T.mm(qb.ap[:, :NC], blk[:, :], sq.ap[:, :NC], True, True, [sq.res, konst], [qb.res])

            def qk_fin(qf, qb, outs, gcol):
                T.act(qb.ap[:, :NC], qb.ap[:, :NC], AF.Ln, [qb.res], [qb.res], scale=1.0 / 64, bias=EPS)
                T.act(qb.ap[:, :NC], qb.ap[:, :NC], AF.Exp, [qb.res], [qb.res], scale=-0.5)
                for (oap, lo, hi, c0, c1, ores) in outs:
                    T.stt(oap, qf.ap[lo:hi, c0:c1], cst_sb[lo:hi, gcol:gcol + 1], qb.ap[lo:hi, c0:c1], ALU.mult, ALU.mult, [qf.res, qb.res, cstr], ores)
            qbanks = C.qk
            sqs = [qk_sq(qf_) for qf_ in qfs]
            qk_mm(sqs[0], qbanks[0])
            sqk = qk_sq(kf)
            for j in range(1, 4):
                qk_mm(sqs[j], qbanks[j])
            qk_mm(sqk, qbanks[4])
            for j in range(4):
                qk_fin(qfs[j], qbanks[j], [(qn[:, j, :NC], 0, 128, 0, NC, [qnr[j]])], C_GQ)
            kouts = [(kT[0][0:64, 128:128 + NC], 0, 64, 0, NC, [kTc]), (kT[1][64:128, 128:128 + NC], 64, 128, 0, NC, [kTc])]
            if last:
                kouts.append((aux_sb[:, A_KP:A_KP + 128], 0, 128, 64, 192, [auxr]))
                kouts.append((aux_sb[:, A_KS + 112:A_KS + 128], 0, 128, 208, 224, [auxr]))
            qk_fin(kf, qbanks[4], kouts, C_GK)
            nA = 4 if not last else 2
            tps = [vTc, vTh, konst]
            for m in range(nA):
                T.tr(tp[:, m * 128:(m + 1) * 128], vT[:, 64 + 128 * m:64 + 128 * (m + 1)], identb[:, :], tps, [tpr[0]])
                T.tr(tp[:, (4 + m) * 128:(5 + m) * 128], vT[:, 128 * m:128 * (m + 1)], identb[:, :], tps, [tpr[0]])
            if last:
                T.tr(tp[:, 2 * 128:3 * 128], vT[:, 64 + 208:64 + 336], identb[:, :], tps, [tpr[0]])
            for m in range(nA):
                TA = 4 * gi + m
                T.copy(ACT, VA[0][:, TA % 6, 0:64], tp[:, m * 128:m * 128 + 64], [tpr[0]], [VAr[TA % 6]])
                T.copy(DVE, VA[1][:, TA % 6, 64:128], tp[:, m * 128 + 64:(m + 1) * 128], [tpr[0]], [VAr[TA % 6]])
                T.copy(ACT, VB[0][:, TA % 4, 0:64], tp[:, (4 + m) * 128:(4 + m) * 128 + 64], [tpr[0]], [VBr[TA % 4]])
                T.copy(DVE, VB[1][:, TA % 4, 64:128], tp[:, (4 + m) * 128 + 64:(5 + m) * 128], [tpr[0]], [VBr[TA % 4]])
            if last:
                T.copy(ACT, Vs[0][:, 0:64], tp[:, 256:320], [tpr[0]], [Vsr])
                T.copy(DVE, Vs[1][:, 64:128], tp[:, 320:384], [tpr[0]], [Vsr])
            L = 16 + NC
            for g in range(4):
                Ug = u[:, g, :]
                cur = Ug
                cres = [uh[g], uc[g]]
                shift = 1
                for stg_ in range(g + 1):
                    nx = pscr()
                    lo = 2 ** (stg_ + 1) - 1
                    T.tt(POOL, nx.ap[:, lo:L], cur[:, lo:L], cur[:, lo - shift:L - shift], ALU.add, cres, [nx.res])
                    cur = nx.ap
                    cres = [nx.res]
                    shift *= 2
                wv_ = float(2 ** (g + 1))
                sc = pscr()
                T.ts(POOL, sc.ap[:, 16:L], cur[:, 16:L], 1.0 / wv_, ALU.mult, cres, [sc.res], s2=0.0, op1=ALU.add)
                if gi == 0:
                    T.tt(POOL, sc.ap[:, 208:224], cur[:, 208:224], cst_sb[:, C_INVC + 16 * g:C_INVC + 16 * g + 16], ALU.mult, cres + [cstr], [sc.res])
                T.tt(POOL, dp[:, g, :NC], sc.ap[:, 16:L], Ug[:, 16:L], ALU.subtract, [sc.res, uc[g]], [dpr[g]])
                if last:
                    T.copy(DVE, aux_sb[:, A_PP + 15 * g:A_PP + 15 * g + 15], Ug[:, 16 + 177:16 + 192], [uc[g]], [auxr])
                    T.copy(DVE, aux_sb[:, A_PS + 15 * g:A_PS + 15 * g + 15], Ug[:, 16 + 209:16 + 224], [uc[g]], [auxr])
            yield
            nch = 8 if not last else 3
            prev = None
            for lc in range(nch):
                c = 8 * gi + lc
                if c < 2:
                    continue
                ev = (c % 2 == 0)
                k0 = lambda g, lc=lc: kT[g][:, 64 * lc:64 * lc + 128]
                if ev:
                    sA = ((c - 2) // 2) % 6
                    v0 = lambda g, sA=sA: VA[g][:, sA, :]
                    v0res = [VAr[sA]]
                else:
                    sB = ((c - 1) // 2) % 4
                    v0 = lambda g, sB=sB: VB[g][:, sB, :]
                    v0res = [VBr[sB]]
                if gi == 0 and c in (3, 4):
                    mb = fscr()
                    mcol = C_NMA if c == 3 else C_NMH
                    T.ts(DVE, mb.ap[:, 0:512], bia_sb[:, 0:512], cst_sb[:, mcol:mcol + 1], ALU.add, [cstr, biar], [mb.res])
                    b0, b0res = mb.ap[:, 0:512], mb.res
                else:
                    b0, b0res = bia_sb[:, 0:512], biar
                if ev:
                    lo, hi = 0, 64
                    k1 = lambda g, lc=lc: kT[g][:, 128 + 64 * lc:128 + 64 * lc + 128]
                    s1A = (c // 2) % 6
                    pb_, pr_ = pT1[0], pT1r[0]
                else:
                    lo, hi = 64, 128
                    k1 = lambda g, lc=lc: kT[g][:, 128 + 64 * (lc - 1):128 + 64 * (lc - 1) + 128]
                    s1A = ((c - 1) // 2) % 6
                    pb_, pr_ = pT1[1], pT1r[1]
                v1 = lambda g, s1A=s1A: VA[g][:, s1A, :]
                st_ = attn_scores(C, 64 * lc, 64, k0, [], v0, v0res, b0, b0res, lo, hi, k1, v1, [VAr[s1A]], bia_sb[:, 512:1024], pb_, pr_)
                if prev is not None:
                    attn_out(prev)
                prev = st_
                yield
            if last:
                st_ = attn_scores(C, 208, 16, lambda g: ckm[g][:, :], [ckvr], lambda g: cvm[g][:, :], [ckvr], bia_sb[:, 1024:1152], biar, 0, 16,
                                  lambda g: kT[g][:, 128 + 208:128 + 336], lambda g: Vs[g][:, :], [Vsr], bia_sb[:, 1152:1280], pT1s, pT1sr)
                attn_out(prev)
                prev = st_
            if prev is not None:
                attn_out(prev)
            w = W.next("poolw")
            wv = w.ap[:, 0:512].rearrange("p (g d) -> p g d", g=4)
            for g in range(4):
                bank = C.mm()
                T.mm(bank.ap[:, :NC], wv[:, g, :], dp[:, g, :NC], True, True, [w.res, dpr[g]], [bank.res])
                T.act(pm[:, g, :NC], bank.ap[:, :NC], AF.Copy, [bank.res, cstr], [pmr[g]], scale=cst_sb[:, C_PSC + g:C_PSC + g + 1])
            W.release()
            if not last:
                for g in range(2):
                    T.copy(POOL, kT[g][:, 0:128], kT[g][:, NC:NC + 128], [kTc], [kTh])
                T.copy(POOL, vT[:, 0:64], vT[:, NC:NC + 64], [vTc], [vTh])
                for g in range(4):
                    T.copy(POOL, u[:, g, 0:16], u[:, g, NC:NC + 16], [uc[g]], [uh[g]])
            yield
            for pp in range(4):
                w = W.next(f"out{pp}")
                wv = w.ap.rearrange("p (k n) -> p k n", k=8)
                for t in range(2):
                    n = 2 * pp + t
                    bank = C.mm()
                    proj8(bank, wv, t * 128, lambda kc: (aT[:, kc, :NC], aTr) if kc < 4 else (pm[:, kc - 4, :NC], pmr[kc - 4]), NC, w.res)
                    T.tt(DVE, X[:, n, :NC], bank.ap[:, :NC], X[:, n, :NC], ALU.add, [bank.res, Xr[n]], [Xr[n]])
                W.release()
                if pp == 1:
                    yield

        def layer1(gi, X, Xr, NC):
            last = gi == 8
            norm(X, Xr, NC, True, True, gbase=C_MIX1)
            xrhs = lambda kc: (xb[:, kc, :NC], xbr[kc])
            wh = None
            pendB = None

            def stageB(f, ch, tb, y0, cw2):
                T.stt(y0.ap[:, :NC], ch.ap[:, 2:2 + NC], cw2, y0.ap[:, :NC], ALU.mult, ALU.add, [ch.res, y0.res, cstr], [y0.res])
                T.tt(DVE, by_ap(f)[:, :NC], y0.ap[:, :NC], tb.ap[:, :NC], ALU.mult, [y0.res, tb.res], [byr[f]])
            for f in range(8):
                wa = W.next(f"ca{f}")
                if f % 2 == 0:
                    wh = W.next(f"ch{f // 2}")
                wav = wa.ap.rearrange("p (k n) -> p k n", k=8)
                whv = wh.ap.rearrange("p (k n) -> p k n", k=8)
                bB, bC, bH = mm6(), mm6(), mm6()
                proj8(bC, wav, 128, xrhs, NC, wa.res)
                proj8(bH, whv, (f % 2) * 128, xrhs, NC, wh.res)
                proj8(bB, wav, 0, xrhs, NC, wa.res)
                if f % 2 == 0:
                    W.release(1)
                else:
                    W.release(2)
                ch = fscr()
                T.copy(POOL, ch.ap[:, 0:2], chh[:, f, :], [chhr[f]], [ch.res])
                T.tt(DVE, ch.ap[:, 2:2 + NC], bC.ap[:, :NC], rr[:, :NC], ALU.mult, [bC.res, rrr], [ch.res])
                T.tt(DVE, ch.ap[:, 2:2 + NC], bH.ap[:, :NC], ch.ap[:, 2:2 + NC], ALU.mult, [bH.res, ch.res], [ch.res])
                tb = fscr()
                T.tt(DVE, tb.ap[:, :NC], bB.ap[:, :NC], rbc[:, :NC], ALU.mult, [bB.res, rbcr], [tb.res])
                if last:
                    T.copy(POOL, ch.ap[:, 2 + 206:2 + 208], cach_sb[:, SCT + 2 * f:SCT + 2 * f + 2], [cachr], [ch.res])
                    T.copy(POOL, aux_sb[:, A_CP + 2 * f:A_CP + 2 * f + 2], ch.ap[:, 2 + 190:2 + 192], [ch.res], [auxr])
                    T.copy(POOL, aux_sb[:, A_CS + 2 * f:A_CS + 2 * f + 2], ch.ap[:, 2 + 222:2 + 224], [ch.res], [auxr])
                else:
                    T.copy(POOL, chh[:, f, :], ch.ap[:, NC:NC + 2], [ch.res], [chhr[f]])
                cw = lambda j, f=f: cst_sb[:, C_CW + 8 * j + f:C_CW + 8 * j + f + 1]
                y0 = fscr()
                T.act(y0.ap[:, :NC], ch.ap[:, 0:NC], AF.Copy, [ch.res, cstr], [y0.res], scale=cw(0))
                y1 = fscr()
                T.act(y1.ap[:, :NC], ch.ap[:, 1:1 + NC], AF.Copy, [ch.res, cstr], [y1.res], scale=cw(1))
                T.tt(POOL, y0.ap[:, :NC], y0.ap[:, :NC], y1.ap[:, :NC], ALU.add, [y0.res, y1.res], [y0.res])
                if pendB is not None:
                    stageB(*pendB)
                pendB = (f, ch, tb, y0, cw(2))
            stageB(*pendB)
            cow = []
            started = []
            for pp in range(3):
                w = W.next(f"co{pp}")
                cow.append(w)
                wv = w.ap.rearrange("p (k n) -> p k n", k=8)
                for t in range(2):
                    bank = mm6()
                    for kc in range(7):
                        T.mm(bank.ap[:, :NC], wv[:, kc, t * 128:(t + 1) * 128], by_ap(kc)[:, :NC], kc == 0, False, [w.res, byr[kc]], [bank.res], signal=False)
                    started.append((bank, wv, t, w, 2 * pp + t))
            for (bank, wv, t, w, n) in started:
                T.mm(bank.ap[:, :NC], wv[:, 7, t * 128:(t + 1) * 128], by_ap(7)[:, :NC], False, True, [w.res, byr[7]], [bank.res], signal=True)
                T.tt(DVE, X[:, n, :NC], bank.ap[:, :NC], X[:, n, :NC], ALU.add, [bank.res, Xr[n]], [Xr[n]])
            W.release(3)
            w = W.next("co3")
            wv = w.ap.rearrange("p (k n) -> p k n", k=8)
            for t in range(2):
                n = 6 + t
                bank = mm6()
                proj8(bank, wv, t * 128, lambda kc: (by_ap(kc)[:, :NC], byr[kc]), NC, w.res)
                T.tt(DVE, X[:, n, :NC], bank.ap[:, :NC], X[:, n, :NC], ALU.add, [bank.res, Xr[n]], [Xr[n]])
            W.release()

        def grp_cols(gi):
            return 512 * gi, (512 if gi < 8 else 224)

        load_x(groups[0])
        l0gen = None
        for idx, gi in enumerate(groups):
            n0, NC = grp_cols(gi)
            s = gi % 2
            X, Xr = xt[s], xr[s]
            if l0gen is None:
                for _ in layer0(gi, X, Xr, NC, CF):
                    pass
            else:
                for _ in l0gen:
                    pass
                l0gen = None
            if stop != "l0":
                has_next = idx + 1 < len(groups)
                pre = (lambda: load_x(groups[idx + 1])) if (has_next and idx > 0) else None
                ffn(X, Xr, NC, "a", on_up=pre)
                if stop != "ffn0":
                    layer1(gi, X, Xr, NC)
                    if stop != "l1":
                        if has_next and idx > 0 and INTERLEAVE:
                            gn = groups[idx + 1]
                            l0gen = layer0(gn, xt[gn % 2], xr[gn % 2], grp_cols(gn)[1], CI)

                            def pull(n, g_=l0gen, cnt=[0]):
                                for _ in range(2 if cnt[0] == 0 else 1):
                                    next(g_, None)
                                cnt[0] += 1
                            ffn(X, Xr, NC, "b", on_down=pull, dring=down_int)
                        else:
                            ffn(X, Xr, NC, "b")
            if idx == 0 and len(groups) > 1:
                assert groups[0] % 2 == 0
                load_x(groups[1])
            T.dma(SP, yT[:, :, n0:n0 + NC], X[:, :, 0:NC], f"s_x{s}", reads=Xr)
        T.dma(SP, aux[:, :], aux_sb[:, :], "s_aux", reads=[auxr])
        for k_, v_ in T.dmacnt.items():
            if v_ and SP.known.get(k_, 0) < v_:
                nc.sync.wait_ge(T.sems[k_], v_)
        if order is None:
            return W.rec
        build.stats = dict(nwaits=T.nwaits, seq={e.name: e.seq for e in (PE, ACT, DVE, POOL)})
    return nc


def _rel_bucket(rel):
    half, max_exact = 16, 8
    n = np.abs(rel)
    large = max_exact + (np.log(np.maximum(n, 1).astype(np.float32) / max_exact) / np.float32(np.log(128 / max_exact)) * (half - max_exact)).astype(np.int32)
    large = np.minimum(large, half - 1)
    return np.where(rel > 0, half, 0) + np.where(n < max_exact, n, large)


def _bucket_table():
    return _rel_bucket(np.arange(-400, 401))


def _pack_weights(inp):
    f32 = np.float32
    out = np.zeros((NP_, 128, PW), f32)

    def pk(Wm, cols):
        return np.transpose(Wm.reshape(8, 128, Wm.shape[1])[:, :, cols], (1, 0, 2))
    Win = inp["ev_w_in"][0]
    tiles = []
    for j in range(4):
        tiles.append(list(range(64 * j, 64 * j + 64)) + list(range(64 * (4 + j), 64 * (4 + j) + 64)))
    tiles.append(list(range(512, 640)))
    tiles.append(list(range(640, 768)))
    for g in range(4):
        tiles.append(list(range(768 + 128 * g, 768 + 128 * (g + 1))))
    rowperm = []
    for kc in range(4):
        rowperm += list(range(64 * kc, 64 * kc + 64)) + list(range(64 * (4 + kc), 64 * (4 + kc) + 64))
    rowperm += list(range(512, 1024))
    Wout = inp["ev_w_out"][0][rowperm]
    idx = {n: i for i, (n, _) in enumerate(PIECES)}
    for i in range(5):
        out[idx[f"in{i}"]] = pk(Win, tiles[2 * i] + tiles[2 * i + 1]).reshape(128, PW)
    out[idx["poolw"], :, 0:512] = np.transpose(inp["pool_w"][0], (1, 0, 2)).reshape(128, 512)
    for i in range(4):
        out[idx[f"out{i}"]] = pk(Wout, list(range(256 * i, 256 * (i + 1)))).reshape(128, PW)
    for l, tag in enumerate("ab"):
        W1 = inp["ffn_w1"][l]
        W2 = inp["ffn_w2"][l]
        for i in range(16):
            out[idx[f"w1{tag}{i}"]] = pk(W1, list(range(256 * i, 256 * (i + 1)))).reshape(128, PW)
        W2r = W2.reshape(32, 128, 1024)
        for n in range(8):
            for hh in range(2):
                blk_ = W2r[16 * hh:16 * (hh + 1), :, 128 * n:128 * (n + 1)]
                out[idx[f"w2{tag}{2 * n + hh}"]] = np.transpose(blk_, (1, 0, 2)).reshape(128, PW)
    Wc = inp["conv_w_in"][0]
    for f in range(8):
        out[idx[f"ca{f}"]] = pk(Wc, list(range(128 * f, 128 * (f + 1))) + list(range(1024 + 128 * f, 1024 + 128 * (f + 1)))).reshape(128, PW)
    for h in range(4):
        out[idx[f"ch{h}"]] = pk(Wc, list(range(2048 + 256 * h, 2048 + 256 * (h + 1)))).reshape(128, PW)
    Wco = inp["conv_w_out"][0]
    for i in range(4):
        out[idx[f"co{i}"]] = pk(Wco, list(range(256 * i, 256 * (i + 1)))).reshape(128, PW)
    return out


def _pack_consts(inp, core):
    f32 = np.float32
    c = np.zeros((128, NCST), f32)
    first = (core % 4 == 0)
    for base, v in ((C_MIX0, inp["norm_mix"][0]), (C_FFN0, inp["norm_ffn"][0]), (C_MIX1, inp["norm_mix"][1]), (C_FFN1, inp["norm_ffn"][1])):
        c[:, base:base + 8] = v.reshape(8, 128).T
    c[:, C_PSC:C_PSC + 4] = inp["pool_scale"][0].reshape(4, 128).T
    c[:, C_PSC + 4] = 1.0
    c[:, C_GQ] = np.tile(inp["q_norm"][0], 2)
    c[:, C_GK] = np.tile(inp["k_norm"][0], 2)
    for j in range(3):
        c[:, C_CW + 8 * j:C_CW + 8 * j + 8] = inp["conv_w"][0][j].reshape(8, 128).T
    sk = inp["attn_sinks"][0]
    c[0:64, C_SINK:C_SINK + 4] = sk[0:4][None, :]
    c[64:128, C_SINK:C_SINK + 4] = sk[4:8][None, :]
    if first:
        c[:, C_NMA] = NEG
        c[0:64, C_NMH] = NEG
    for g in range(4):
        w = 2 ** (g + 1)
        for j in range(16):
            c[:, C_INVC + 16 * g + j] = 1.0 / (min(j + 1, w) if first else w)
    c[:, C_ID:C_ID + 128] = np.eye(128, dtype=f32)
    return c


def _pack_bias(inp):
    f32 = np.float32
    rb = inp["rel_bias"]
    bt = _bucket_table()

    def bk(rel):
        return bt[rel + 400]
    out = np.zeros((128, NBIA), f32)
    k = np.arange(128)[:, None]
    q = np.arange(64)[None, :]
    b0 = rb[bk(k - 128 - q)]
    k64 = (np.arange(128) % 64)[:, None]
    b1 = rb[bk(k64 - q)]

    def lay(b, nq):
        return np.transpose(b.reshape(128, nq, 2, 4), (0, 2, 3, 1)).reshape(128, 8 * nq)
    out[:, 0:512] = lay(b0, 64)
    out[:, 512:1024] = lay(b1, 64)
    qs = np.arange(16)[None, :]
    bs0 = rb[bk((896 + k) - (1024 + qs))]
    k16 = np.minimum(np.arange(128), 15)[:, None]
    bs1 = rb[bk(k16 - qs)]
    out[:, 1024:1152] = lay(bs0, 16)
    out[:, 1152:1280] = lay(bs1, 16)
    return out


_NC_CACHE = {}


def _get_nc():
    if "nc" not in _NC_CACHE:
        _NC_CACHE["nc"] = build()
    return _NC_CACHE["nc"]


def make_in_maps(inp):
    f32 = np.float32
    inp = {k: np.asarray(v) for k, v in inp.items()}
    w32 = _pack_weights(inp)
    bia = _pack_bias(inp)
    maps = []
    for core in range(NCORE):
        b, qd = core // 4, core % 4
        s0 = qd * TOK
        rows = np.zeros((NCOL, 1024), f32)
        lo = s0 - HALO
        if lo >= 0:
            rows[0:HALO + TOK] = inp["x_prompt"][b, lo:s0 + TOK]
        else:
            rows[HALO:HALO + TOK] = inp["x_prompt"][b, s0:s0 + TOK]
        rows[HALO + TOK + 16:] = inp["x_sample"][core]
        xin = np.ascontiguousarray(np.transpose(rows.reshape(NCOL, 8, 128), (2, 1, 0)))
        cach = np.zeros((128, 3 * 128 + 64 + 16), f32)
        ck = inp["cache_k"][0, core].reshape(128, 128)
        cv = inp["cache_v"][0, core].reshape(128, 128)
        cach[:, 0:128] = ck.T
        cach[:, 128:256] = cv.T
        cach[:, 256:384] = cv
        sp = inp["state_pool"][0, core]
        spt = np.zeros((128, 4, 16), f32)
        spt[:, :, 1:16] = np.transpose(sp.reshape(15, 4, 128), (2, 1, 0))
        cach[:, 384:448] = spt.reshape(128, 64)
        sc = inp["state_conv"][0, core]
        cach[:, 448:464] = np.transpose(sc.reshape(2, 8, 128), (2, 1, 0)).reshape(128, 16)
        maps.append({"xin": xin, "w32": w32, "cst": _pack_consts(inp, core), "bia": bia, "cach": cach})
    return maps


def assemble(results):
    f32 = np.float32
    y_p = np.zeros((2, 16384, 1024), f32)
    y_s = np.zeros((8, 16, 1024), f32)
    k_p = np.zeros((1, 2, 128, 2, 64), f32); v_p = np.zeros_like(k_p)
    pool_p = np.zeros((1, 2, 15, 512), f32); conv_p = np.zeros((1, 2, 2, 1024), f32)
    k_s = np.zeros((1, 8, 128, 2, 64), f32); v_s = np.zeros_like(k_s)
    pool_s = np.zeros((1, 8, 15, 512), f32); conv_s = np.zeros((1, 8, 2, 1024), f32)
    for core in range(NCORE):
        r = results[core]
        yT = np.asarray(r["yT"]); ax = np.asarray(r["aux"])
        b, qd = core // 4, core % 4
        y_p[b, qd * TOK:(qd + 1) * TOK] = np.transpose(yT[:, :, HALO:HALO + TOK], (2, 1, 0)).reshape(TOK, 1024)
        y_s[core] = np.transpose(yT[:, :, HALO + TOK + 16:], (2, 1, 0)).reshape(16, 1024)
        if qd == 3:
            k_p[0, b] = ax[:, A_KP:A_KP + 128].T.reshape(128, 2, 64)
            v_p[0, b] = ax[:, A_VP:A_VP + 128].T.reshape(128, 2, 64)
            pool_p[0, b] = np.transpose(ax[:, A_PP:A_PP + 60].reshape(128, 4, 15), (2, 1, 0)).reshape(15, 512)
            conv_p[0, b] = np.transpose(ax[:, A_CP:A_CP + 16].reshape(128, 8, 2), (2, 1, 0)).reshape(2, 1024)
        k_s[0, core] = ax[:, A_KS:A_KS + 128].T.reshape(128, 2, 64)
        v_s[0, core] = ax[:, A_VS:A_VS + 128].T.reshape(128, 2, 64)
        pool_s[0, core] = np.transpose(ax[:, A_PS:A_PS + 60].reshape(128, 4, 15), (2, 1, 0)).reshape(15, 512)
        conv_s[0, core] = np.transpose(ax[:, A_CS:A_CS + 16].reshape(128, 8, 2), (2, 1, 0)).reshape(2, 1024)
    return (y_p, y_s, k_p, v_p, pool_p, conv_p, k_s, v_s, pool_s, conv_s)


def kernel(**inputs):
    nc = _get_nc()
    maps = make_in_maps(inputs)
    res = run_bass_kernel_spmd(nc, maps, core_ids=list(range(NCORE)))
    return assemble(res.results)
```

```python
import numpy as np
from contextlib import ExitStack
import concourse.bass as bass
import concourse.mybir as mybir
from concourse.bass_utils import run_bass_kernel_spmd

F32 = mybir.dt.float32
BF16 = mybir.dt.bfloat16
AF = mybir.ActivationFunctionType
ALU = mybir.AluOpType

NCORE = 8
TOK = 4096
HALO = 192
NCOL = HALO + TOK + 32
NGRP = 9
EPS = 1e-6
NEG = -1.0e4
PW = 2048
NW = 8
INTERLEAVE = True
DEFER = True
S_DEFER = set()

C_MIX0, C_FFN0, C_MIX1, C_FFN1, C_PSC, C_GQ, C_GK, C_CW, C_SINK, C_NMA, C_NMH, C_INVC, C_ID = 0, 8, 16, 24, 32, 37, 38, 39, 63, 67, 68, 69, 133
NCST = 261
A_KP, A_VP, A_PP, A_CP, A_KS, A_VS, A_PS, A_CS = 0, 128, 256, 316, 332, 460, 588, 648
AUXW = 664
NBIA = 2 * 512 + 256


def piece_table():
    P = []
    def kc8(base):
        return [(0, 2048, None)]
    for i in range(5):
        P.append((f"in{i}", kc8(C_MIX0)))
    P.append(("poolw", [(0, 2048, None)]))
    for i in range(4):
        P.append((f"out{i}", [(0, 2048, None)]))
    for i in range(16):
        P.append((f"w1a{i}", kc8(C_FFN0)))
    for i in range(16):
        P.append((f"w2a{i}", [(0, 2048, None)]))
    for f in range(8):
        P.append((f"ca{f}", kc8(C_MIX1)))
        if f % 2 == 0:
            P.append((f"ch{f // 2}", kc8(C_MIX1)))
    for i in range(4):
        P.append((f"co{i}", [(0, 2048, None)]))
    for i in range(16):
        P.append((f"w1b{i}", kc8(C_FFN1)))
    for i in range(16):
        P.append((f"w2b{i}", [(0, 2048, None)]))
    return P


PIECES = piece_table()
NP_ = len(PIECES)
PIDX = {n: i for i, (n, _) in enumerate(PIECES)}
if DEFER:
    S_DEFER = set(PIDX[f"{w_}{i_}"] for w_ in ("w1a", "w2a", "w1b", "w2b") for i_ in range(1, 16, 2))


class Res:
    __slots__ = ("name", "w", "r")

    def __init__(self, name):
        self.name = name
        self.w = None
        self.r = {}


class Eng:
    def __init__(self, name, h, key):
        self.name = name
        self.h = h
        self.key = key
        self.seq = 0
        self.known = {}


class Slot:
    __slots__ = ("ap", "res", "t")

    def __init__(self, t, res):
        self.t = t
        self.ap = t
        self.res = res


class Tracker:
    def __init__(self, nc, es):
        self.nc = nc
        self.es = es
        self.sems = {}
        self.snap = {}
        self.dmacnt = {}
        self.PE = self._eng("pe", nc.tensor)
        self.ACT = self._eng("act", nc.scalar)
        self.DVE = self._eng("dve", nc.vector)
        self.POOL = self._eng("pool", nc.gpsimd)
        self.SP = Eng("sp", nc.sync, "q_sp")
        self.nwaits = 0

    def _eng(self, name, h):
        key = "e_" + name
        self.sems[key] = self.es.enter_context(self.nc.semaphore(key))
        return Eng(name, h, key)

    def dsem(self, key):
        if key not in self.sems:
            self.sems[key] = self.es.enter_context(self.nc.semaphore(key))
            self.dmacnt[key] = 0
        return key

    def _deps(self, E, reads, writes):
        raw = {}
        war = {}

        def add(d, tgt):
            if d is None:
                return
            k, v = d
            if tgt.get(k, 0) < v:
                tgt[k] = v
        for r in reads:
            add(r.w, raw)
        for w in writes:
            add(w.w, raw)
            for k, v in w.r.items():
                add((k, v), war)
        deps = dict(raw)
        for k, v in war.items():
            if k == E.key and E is not self.POOL:
                continue
            if deps.get(k, 0) < v:
                deps[k] = v
        if E is self.PE and E.key in deps:
            del deps[E.key]
        for k, v in deps.items():
            if E.known.get(k, 0) >= v:
                continue
            E.h.wait_ge(self.sems[k], v)
            self.nwaits += 1
            E.known[k] = v
            sn = self.snap.get((k, v))
            if sn:
                for k2, v2 in sn.items():
                    if E.known.get(k2, 0) < v2:
                        E.known[k2] = v2

    def op(self, E, fn, reads=(), writes=(), signal=True):
        self._deps(E, reads, writes)
        ins = fn()
        stamp = E.seq + 1
        if signal:
            E.seq = stamp
            ins.then_inc(self.sems[E.key], 1)
            sn = dict(E.known)
            self.snap[(E.key, stamp)] = sn
        for r in reads:
            if r.r.get(E.key, 0) < stamp:
                r.r[E.key] = stamp
        for w in writes:
            w.w = (E.key, stamp)
            w.r = {}
        return ins

    def dma(self, Q, out, in_, semkey, reads=(), writes=()):
        self.dsem(semkey)
        self._deps(Q, reads, writes)
        ins = Q.h.dma_start(out=out, in_=in_)
        self.dmacnt[semkey] += 16
        val = self.dmacnt[semkey]
        ins.then_inc(self.sems[semkey], 16)
        self.snap[(semkey, val)] = dict(Q.known)
        for r in reads:
            if r.r.get(semkey, 0) < val:
                r.r[semkey] = val
        for w in writes:
            w.w = (semkey, val)
            w.r = {}

    def barrier(self):
        engs = [self.PE, self.ACT, self.DVE, self.POOL]
        for E in engs + [self.SP]:
            for O in engs:
                if O is E or O.seq == 0:
                    continue
                if E.known.get(O.key, 0) < O.seq:
                    E.h.wait_ge(self.sems[O.key], O.seq)
                    E.known[O.key] = O.seq
            for k, v in self.dmacnt.items():
                if v and E.known.get(k, 0) < v:
                    E.h.wait_ge(self.sems[k], v)
                    E.known[k] = v

    def act(self, out, in_, func, reads, writes, scale=None, bias=None):
        kw = {}
        if scale is not None:
            kw["scale"] = scale
        if bias is not None:
            kw["bias"] = bias
        return self.op(self.ACT, lambda: self.nc.scalar.activation(out=out, in_=in_, func=func, **kw), reads, writes)

    def tt(self, E, out, in0, in1, op, reads, writes):
        return self.op(E, lambda: E.h.tensor_tensor(out=out, in0=in0, in1=in1, op=op), reads, writes)

    def ts(self, E, out, in0, s1, op0, reads, writes, s2=None, op1=None):
        if op1 is None:
            return self.op(E, lambda: E.h.tensor_scalar(out=out, in0=in0, scalar1=s1, scalar2=None, op0=op0), reads, writes)
        return self.op(E, lambda: E.h.tensor_scalar(out=out, in0=in0, scalar1=s1, scalar2=s2, op0=op0, op1=op1), reads, writes)

    def stt(self, out, in0, scalar, in1, op0, op1, reads, writes):
        return self.op(self.DVE, lambda: self.nc.vector.scalar_tensor_tensor(out=out, in0=in0, scalar=scalar, in1=in1, op0=op0, op1=op1), reads, writes)

    def copy(self, E, out, in_, reads, writes):
        if E is self.ACT:
            return self.act(out, in_, AF.Copy, reads, writes)
        return self.op(E, lambda: E.h.tensor_copy(out=out, in_=in_), reads, writes)

    def memset(self, E, ap, val, writes):
        return self.op(E, lambda: E.h.memset(ap, val), (), writes)

    def mm(self, out, lhsT, rhs, start, stop, reads, writes, signal=True):
        return self.op(self.PE, lambda: self.nc.tensor.matmul(out, lhsT=lhsT, rhs=rhs, start=start, stop=stop), reads, writes, signal)

    def tr(self, out, in_, ident, reads, writes):
        return self.op(self.PE, lambda: self.nc.tensor.transpose(out, in_, ident), reads, writes)


class Ring:
    def __init__(self, slots):
        self.slots = slots
        self.i = 0

    def __call__(self):
        s = self.slots[self.i % len(self.slots)]
        self.i += 1
        return s


def build(groups=None, stop=None):
    order = _build(groups, stop, None)
    return _build(groups, stop, order)


def _build(groups, stop, order):
    if groups is None:
        groups = list(range(NGRP))
    nc = bass.Bass("TRN2", target_bir_lowering=False)
    xin = nc.dram_tensor("xin", [128, 8, NCOL], F32, kind="ExternalInput").ap()
    w32 = nc.dram_tensor("w32", [NP_, 128, PW], F32, kind="ExternalInput").ap()
    cst = nc.dram_tensor("cst", [128, NCST], F32, kind="ExternalInput").ap()
    bia = nc.dram_tensor("bia", [128, NBIA], F32, kind="ExternalInput").ap()
    cach = nc.dram_tensor("cach", [128, 3 * 128 + 64 + 16], F32, kind="ExternalInput").ap()
    yT = nc.dram_tensor("yT", [128, 8, NCOL], F32, kind="ExternalOutput").ap()
    aux = nc.dram_tensor("aux", [128, AUXW], F32, kind="ExternalOutput").ap()
    wbf = nc.dram_tensor("wbf", [NP_, 128, PW], BF16, kind="Internal").ap()

    with ExitStack() as es:
        T = Tracker(nc, es)
        PE, ACT, DVE, POOL, SP = T.PE, T.ACT, T.DVE, T.POOL, T.SP

        def sb(name, shape, dt):
            return es.enter_context(nc.sbuf_tensor(name, shape, dt))

        def ps(name, shape, dt):
            return es.enter_context(nc.psum_tensor(name, shape, dt))

        xt = [sb(f"x{i}", [128, 8, 512], F32) for i in range(2)]
        xr = [[Res(f"x{i}_{k}") for k in range(8)] for i in range(2)]
        xb = sb("xb", [128, 8, 512], BF16)
        xbr = [Res(f"xb{k}") for k in range(8)]
        rbc = sb("rbc", [128, 512], F32); rbcr = Res("rbc")
        rr = sb("rr", [128, 512], F32); rrr = Res("rr")
        qn = sb("qn", [128, 4, 512], BF16); qnr = [Res(f"qn{j}") for j in range(4)]
        kT = [sb(f"kT{g}", [128, 768], BF16) for g in range(2)]
        kTh = Res("kTh"); kTc = Res("kTc")
        vT = sb("vT", [128, 64 + 512], BF16); vTh = Res("vTh"); vTc = Res("vTc")
        VA = [sb(f"VA{g}", [128, 6, 128], BF16) for g in range(2)]; VAr = [Res(f"VA{s}") for s in range(6)]
        VB = [sb(f"VB{g}", [128, 4, 128], BF16) for g in range(2)]; VBr = [Res(f"VB{s}") for s in range(4)]
        Vs = [sb(f"Vs{g}", [128, 128], BF16) for g in range(2)]; Vsr = Res("Vs")
        u = sb("u", [128, 4, 528], F32); uh = [Res(f"uh{g}") for g in range(4)]; uc = [Res(f"uc{g}") for g in range(4)]
        dp = sb("dp", [128, 4, 512], BF16); dpr = [Res(f"dp{g}") for g in range(4)]
        pm = sb("pm", [128, 4, 512], BF16); pmr = [Res(f"pm{g}") for g in range(4)]
        byr = dpr + pmr
        by_ap = lambda f: dp[:, f, :] if f < 4 else pm[:, f - 4, :]
        stg = [sb(f"stg{i}", [128, PW], F32) for i in range(2)]; stgr = [Res(f"stg{i}") for i in range(2)]
        aT = sb("aT", [128, 4, 512], BF16); aTr = Res("aT")
        pT1 = [sb("pT1t", [128, 512], BF16), sb("pT1b", [128, 512], BF16)]; pT1r = [Res("pT1t"), Res("pT1b")]
        pT1s = sb("pT1s", [128, 128], BF16); pT1sr = Res("pT1s")
        hid = sb("hid", [128, 32, 512], BF16); hidr = [Res(f"hid{k}") for k in range(32)]
        chh = sb("chh", [128, 8, 2], F32); chhr = [Res(f"chh{k}") for k in range(8)]
        wring = [Slot(sb(f"w{i}", [128, PW], BF16), Res(f"w{i}")) for i in range(NW)]
        fscr = Ring([Slot(sb(f"fs{i}", [128, 528], F32), Res(f"fs{i}")) for i in range(9)])
        pscr = Ring([Slot(sb(f"pscr{i}", [128, 528], F32), Res(f"pscr{i}")) for i in range(4)])
        bscr = Ring([Slot(sb(f"bs{i}", [128, 512], BF16), Res(f"bs{i}")) for i in range(4)])
        cst_sb = sb("cst_sb", [128, NCST], F32); cstr = Res("cst")
        bia_sb = sb("bia_sb", [128, NBIA], F32); biar = Res("bia")
        cach_sb = sb("cach_sb", [128, 3 * 128 + 64 + 16], F32); cachr = Res("cach")
        ckm = [sb(f"ckm{g}", [128, 128], BF16) for g in range(2)]
        cvm = [sb(f"cvm{g}", [128, 128], BF16) for g in range(2)]
        ckvr = Res("ckv")
        aux_sb = sb("aux_sb", [128, AUXW], F32); auxr = Res("aux")
        onesb = sb("onesb", [128, 128], BF16)
        blk = sb("blk", [128, 128], BF16)
        hones = [sb(f"hones{g}", [128, 128], BF16) for g in range(2)]
        identb = sb("identb", [128, 128], BF16)
        esink = sb("esink", [128, 4], F32)
        konst = Res("konst")
        banks = [Slot(ps(f"pb{i}", [128, 512], F32), Res(f"pb{i}")) for i in range(7)]
        tp = ps("tp", [128, 1024], BF16); tpr = [Res(f"tp{i}") for i in range(8)]
        misc = banks[3]

        class Cfg:
            pass
        CF = Cfg()
        CF.mm = Ring([banks[0], banks[1], banks[2]]); CF.misc = banks[3]
        CF.st0 = Ring([banks[4], banks[0]]); CF.st1 = Ring([banks[5], banks[1]]); CF.od = Ring([banks[6], banks[3]])
        CF.qk = [banks[0], banks[1], banks[2], banks[4], banks[5]]
        CI = Cfg()
        CI.mm = Ring([banks[0], banks[1]]); CI.misc = banks[3]
        CI.st0 = Ring([banks[4]]); CI.st1 = Ring([banks[5]]); CI.od = Ring([banks[3]])
        CI.qk = [banks[0], banks[1], banks[4], banks[5], banks[3]]
        down_int = Ring([banks[2], banks[6]])
        st0, st1 = banks[4], banks[5]
        mm3 = Ring([banks[0], banks[1], banks[2]])
        mm6 = Ring([banks[0], banks[1], banks[2], banks[4], banks[5], banks[6]])

        T.dma(SP, cst_sb[:], cst[:, :], "s_setup", writes=[cstr])
        T.dma(SP, bia_sb[:], bia[:, :], "s_setup", writes=[biar])
        T.dma(SP, cach_sb[:], cach[:, :], "s_setup", writes=[cachr])
        tot = T.dmacnt["s_setup"]
        for r_ in (cstr, biar, cachr):
            r_.w = ("s_setup", tot)
        T.memset(DVE, onesb[:], 1.0, [konst])
        T.memset(DVE, blk[:], 0.0, [konst])
        T.memset(DVE, blk[0:64, 0:64], 1.0, [konst])
        T.memset(DVE, blk[64:128, 64:128], 1.0, [konst])
        for g in range(2):
            T.memset(DVE, hones[g][:], 0.0, [konst])
            T.memset(DVE, hones[g][:, g * 64:(g + 1) * 64], 1.0, [konst])
        T.copy(DVE, identb[:], cst_sb[:, C_ID:C_ID + 128], [cstr], [konst])
        T.act(esink[:], cst_sb[:, C_SINK:C_SINK + 4], AF.Exp, [cstr], [konst])
        for g in range(2):
            T.memset(POOL, kT[g][:], 0.0, [kTh, kTc])
            T.memset(POOL, VA[g][:], 0.0, VAr)
            T.memset(POOL, VB[g][:], 0.0, VBr)
            T.memset(POOL, Vs[g][:], 0.0, [Vsr])
            T.memset(POOL, ckm[g][:], 0.0, [ckvr])
            T.memset(POOL, cvm[g][:], 0.0, [ckvr])
            T.memset(POOL, pT1[g][:], 0.0, [pT1r[g]])
        T.memset(POOL, pT1s[:], 0.0, [pT1sr])
        T.memset(POOL, vT[:], 0.0, [vTh, vTc])
        T.memset(POOL, u[:], 0.0, uh + uc)
        T.memset(POOL, chh[:], 0.0, chhr)
        T.memset(POOL, aT[:], 0.0, [aTr])
        T.memset(POOL, aux_sb[:], 0.0, [auxr])
        CK, CVT, CV, SPT, SCT = 0, 128, 256, 384, 448
        for g in range(2):
            T.copy(DVE, ckm[g][g * 64:(g + 1) * 64, :], cach_sb[g * 64:(g + 1) * 64, CK:CK + 128], [cachr], [ckvr])
            T.copy(DVE, cvm[g][:, g * 64:(g + 1) * 64], cach_sb[:, CV + g * 64:CV + (g + 1) * 64], [cachr], [ckvr])
        T.copy(DVE, aux_sb[:, A_KS:A_KS + 112], cach_sb[:, CK + 16:CK + 128], [cachr], [auxr])
        T.copy(DVE, aux_sb[:, A_VS:A_VS + 112], cach_sb[:, CVT + 16:CVT + 128], [cachr], [auxr])

        wbfr = [Res(f"wbf{p}") for p in range(NP_)]

        NSTG = 4

        def stage_ap(i):
            if i < 2:
                return xt[1][:, 4 * i:4 * i + 4, :].rearrange("p a b -> p (a b)")
            return stg[i - 2][:, :]

        def stage_res(i):
            return xr[1][4 * i:4 * i + 4] if i < 2 else [stgr[i - 2]]

        class WS:
            def __init__(self):
                self.issued = 0
                self.loaded = 0
                self.taken = 0
                self.done = 0
                self.cnt = 0
                self.nconv = 0
                self.rec = [] if order is None else None
                self.order = order or []
                self.total = len(self.order)
                self.first = {}
                for i_, p_ in enumerate(self.order):
                    self.first.setdefault(p_, i_)
                assert order is None or sorted(self.first, key=self.first.get) == list(range(NP_))
                self.occ = []
                c_ = {}
                for p_ in self.order:
                    self.occ.append(c_.get(p_, 0))
                    c_[p_] = c_.get(p_, 0) + 1
                self.late = [i_ for i_, p_ in enumerate(self.order) if self.occ[i_] == 1 and p_ in S_DEFER]
                self.lloaded = 0
                self.lconv = 0

            def _convert(self, p, slot, st_i, store):
                sf = stage_ap(st_i)
                sres = stage_res(st_i)
                wsl = wring[slot]
                for (a_, ln_, col) in PIECES[p][1]:
                    E = DVE if self.cnt % 2 == 0 else ACT
                    self.cnt += 1
                    if col is None:
                        T.copy(E, wsl.ap[:, a_:a_ + ln_], sf[:, a_:a_ + ln_], sres, [wsl.res])
                    elif E is DVE:
                        T.ts(DVE, wsl.ap[:, a_:a_ + ln_], sf[:, a_:a_ + ln_], cst_sb[:, col:col + 1], ALU.mult, sres + [cstr], [wsl.res])
                    else:
                        T.act(wsl.ap[:, a_:a_ + ln_], sf[:, a_:a_ + ln_], AF.Copy, sres + [cstr], [wsl.res], scale=cst_sb[:, col:col + 1])
                if store:
                    T.dma(SP, wbf[p], wsl.ap[:, :], f"s_wo{slot}", reads=[wsl.res], writes=[wbfr[p]])

            def _stage_loads(self):
                while self.loaded < NP_ and self.loaded < self.nconv + NSTG:
                    p = self.loaded
                    T.dma(SP, stage_ap(p % NSTG), w32[p], f"s_stg{p % NSTG}", writes=stage_res(p % NSTG))
                    self.loaded += 1
                while self.lloaded < len(self.late) and self.nconv >= NP_ and self.lloaded < self.lconv + 2:
                    p = self.order[self.late[self.lloaded]]
                    st_i = 2 + self.lloaded % 2
                    T.dma(SP, stage_ap(st_i), w32[p], f"s_stg{st_i}", writes=stage_res(st_i))
                    self.lloaded += 1

            def _pump(self):
                self._stage_loads()
                while self.issued < self.total and self.issued < self.done + NW:
                    s_ = self.issued
                    p_ = self.order[s_]
                    if self.first[p_] == s_:
                        assert p_ == self.nconv
                        self._convert(p_, s_ % NW, p_ % NSTG, p_ not in S_DEFER)
                        self.nconv += 1
                        self.issued += 1
                        self._stage_loads()
                    elif self.occ[s_] == 1 and p_ in S_DEFER:
                        j_ = self.lconv
                        assert self.late[j_] == s_ and self.lloaded > j_
                        self._convert(p_, s_ % NW, 2 + j_ % 2, True)
                        self.lconv += 1
                        self.issued += 1
                        self._stage_loads()
                    else:
                        sl = wring[s_ % NW]
                        T.dma(SP, sl.ap[:, :], wbf[p_], f"s_w{s_ % NW}", reads=[wbfr[p_]], writes=[sl.res])
                        self.issued += 1

            def next(self, name):
                if self.rec is not None:
                    self.rec.append(PIDX[name])
                    self.taken += 1
                    return wring[(self.taken - 1) % NW]
                self._pump()
                s_ = self.taken
                assert PIECES[self.order[s_]][0] == name, (PIECES[self.order[s_]][0], name)
                assert s_ < self.issued
                self.taken += 1
                return wring[s_ % NW]

            def release(self, n=1):
                if self.rec is not None:
                    return
                self.done += n
                self._pump()
        W = WS()

        def load_x(gi):
            n0 = 512 * gi
            NC = 512 if gi < 8 else 224
            s = gi % 2
            T.dma(SP, xt[s][:, :, 0:NC], xin[:, :, n0:n0 + NC], f"s_x{s}", writes=xr[s])

        def norm(X, Xr, NC, want_r, want_rr, misc=misc, gbase=None):
            for kc in range(8):
                sq = bscr()
                T.act(sq.ap[:, :NC], X[:, kc, :NC], AF.Square, [Xr[kc]], [sq.res])
                T.ts(DVE, xb[:, kc, :NC], X[:, kc, :NC], cst_sb[:, gbase + kc:gbase + kc + 1], ALU.mult, [Xr[kc], cstr], [xbr[kc]])
                T.mm(misc.ap[:, :NC], onesb[:, :], sq.ap[:, :NC], kc == 0, kc == 7, [sq.res, konst], [misc.res])
            ln = fscr()
            T.act(ln.ap[:, :NC], misc.ap[:, :NC], AF.Ln, [misc.res], [ln.res], scale=1.0 / 1024, bias=EPS)
            if want_r:
                T.act(rbc[:, :NC], ln.ap[:, :NC], AF.Exp, [ln.res], [rbcr], scale=-0.5)
            if want_rr:
                T.act(rr[:, :NC], ln.ap[:, :NC], AF.Exp, [ln.res], [rrr], scale=-1.0)

        def proj8(bank, wv, c0, rhs_fn, NC, wres):
            for kc in range(8):
                rap, rres = rhs_fn(kc)
                T.mm(bank.ap[:, :NC], wv[:, kc, c0:c0 + 128], rap, kc == 0, kc == 7, [wres, rres], [bank.res], signal=(kc == 7))

        def ffn(X, Xr, NC, tag, on_up=None, on_down=None, dring=None):
            norm(X, Xr, NC, False, True, gbase=(C_FFN0 if tag == "a" else C_FFN1))
            for pp in range(16):
                w = W.next(f"w1{tag}{pp}")
                wv = w.ap.rearrange("p (k n) -> p k n", k=8)
                for t in range(2):
                    ht = 2 * pp + t
                    bank = mm6()
                    proj8(bank, wv, t * 128, lambda kc: (xb[:, kc, :NC], xbr[kc]), NC, w.res)
                    a = fscr()
                    T.act(a.ap[:, :NC], bank.ap[:, :NC], AF.Relu, [bank.res], [a.res])
                    T.stt(hid[:, ht, :NC], bank.ap[:, :NC], 0.0, a.ap[:, :NC], ALU.max, ALU.mult, [bank.res, a.res], [hidr[ht]])
                W.release()
                if pp == 7 and on_up is not None:
                    on_up()
            for n in range(8):
                bank = mm6() if dring is None else dring()
                for hh in range(2):
                    w = W.next(f"w2{tag}{2 * n + hh}")
                    wv = w.ap.rearrange("p (k n) -> p k n", k=16)
                    for kl in range(16):
                        kc = 16 * hh + kl
                        T.mm(bank.ap[:, :NC], wv[:, kl, :], hid[:, kc, :NC], kc == 0, kc == 31, [w.res, hidr[kc]], [bank.res], signal=(kl == 15))
                    if hh == 0 and on_down is not None:
                        W.release(1)
                        on_down(n)
                t_ = fscr()
                T.tt(DVE, t_.ap[:, :NC], bank.ap[:, :NC], rr[:, :NC], ALU.mult, [bank.res, rrr], [t_.res])
                addE = DVE if (on_down is not None or n % 2 == 1) else POOL
                T.tt(addE, X[:, n, :NC], X[:, n, :NC], t_.ap[:, :NC], ALU.add, [Xr[n], t_.res], [Xr[n]])
                W.release(1 if on_down is not None else 2)
                if on_down is not None:
                    on_down(n)

        def attn_scores(C, qc0, nq, k0, k0res, v0, v0res, b0, b0res, lo, hi, k1, v1, v1res, b1, pbuf, pres):
            Wd = 4 * nq
            st0, st1 = C.st0(), C.st1()
            qres = qnr + [kTh, kTc]
            q3 = qn[:, 0:4, qc0:qc0 + nq]
            for g in range(2):
                T.mm(st0.ap[:, g * Wd:(g + 1) * Wd].rearrange("p (j q) -> p j q", j=4), k0(g), q3, True, True, qres + k0res, [st0.res], signal=(g == 1))
            for g in range(2):
                T.mm(st1.ap[:, g * Wd:(g + 1) * Wd].rearrange("p (j q) -> p j q", j=4), k1(g), q3, True, True, qres, [st1.res], signal=(g == 1))
            s0 = fscr()
            T.stt(s0.ap[:, :2 * Wd], st0.ap[:, :2 * Wd], 0.125, b0, ALU.mult, ALU.add, [st0.res, b0res], [s0.res])
            p0 = bscr()
            T.act(p0.ap[:, :2 * Wd], s0.ap[:, :2 * Wd], AF.Exp, [s0.res], [p0.res])
            s1 = fscr()
            T.stt(s1.ap[lo:hi, :2 * Wd], st1.ap[lo:hi, :2 * Wd], 0.125, b1[lo:hi, :], ALU.mult, ALU.add, [st1.res, biar], [s1.res])
            T.act(pbuf[lo:hi, :2 * Wd], s1.ap[lo:hi, :2 * Wd], AF.Exp, [s1.res], [pres])
            return (C, qc0, nq, v0, v0res, v1, v1res, pbuf, pres, p0)

        def attn_out(state):
            C, qc0, nq, v0, v0res, v1, v1res, pbuf, pres, p0 = state
            Wd = 4 * nq
            od = C.od()
            mms = [(v0(g), p0.ap[:, g * Wd:(g + 1) * Wd], [p0.res] + v0res) for g in range(2)] + \
                  [(v1(g), pbuf[:, g * Wd:(g + 1) * Wd], [pres] + v1res) for g in range(2)]
            for i, (l_, r_, rs) in enumerate(mms):
                T.mm(od.ap[:, 0:Wd], l_, r_, i == 0, i == 3, rs, [od.res], signal=False)
            dms = [(hones[g][:, :], p0.ap[:, g * Wd:(g + 1) * Wd], [p0.res, konst]) for g in range(2)] + \
                  [(hones[g][:, :], pbuf[:, g * Wd:(g + 1) * Wd], [pres, konst]) for g in range(2)]
            for i, (l_, r_, rs) in enumerate(dms):
                T.mm(od.ap[:, 256:256 + Wd], l_, r_, i == 0, i == 3, rs, [od.res], signal=(i == 3))
            ds = fscr()
            d3 = ds.ap[:, 0:Wd].rearrange("p (j q) -> p j q", j=4)
            T.tt(DVE, d3, od.ap[:, 256:256 + Wd].rearrange("p (j q) -> p j q", j=4),
                 esink[:, 0:4].unsqueeze(2).broadcast_to([128, 4, nq]), ALU.add, [od.res, konst], [ds.res])
            T.act(ds.ap[:, 0:Wd], ds.ap[:, 0:Wd], AF.Ln, [ds.res], [ds.res])
            T.act(ds.ap[:, 0:Wd], ds.ap[:, 0:Wd], AF.Exp, [ds.res], [ds.res], scale=-1.0)
            T.tt(DVE, aT[:, 0:4, qc0:qc0 + nq], od.ap[:, 0:Wd].rearrange("p (j q) -> p j q", j=4), d3, ALU.mult, [od.res, ds.res], [aTr])

        def layer0(gi, X, Xr, NC, C):
            last = gi == 8
            norm(X, Xr, NC, True, False, C.misc, gbase=C_MIX0)
            yield
            xrhs = lambda kc: (xb[:, kc, :NC], xbr[kc])
            qfs = []
            for half in range(2):
                w = W.next(f"in{half}")
                wv = w.ap.rearrange("p (k n) -> p k n", k=8)
                for t in range(2):
                    bank = C.mm()
                    proj8(bank, wv, t * 128, xrhs, NC, w.res)
                    qf = fscr()
                    T.tt(DVE, qf.ap[:, :NC], bank.ap[:, :NC], rbc[:, :NC], ALU.mult, [bank.res, rbcr], [qf.res])
                    qfs.append(qf)
                W.release()
                yield
            w = W.next("in2")
            wv = w.ap.rearrange("p (k n) -> p k n", k=8)
            bank = C.mm()
            proj8(bank, wv, 0, xrhs, NC, w.res)
            kf = fscr()
            T.tt(DVE, kf.ap[:, :NC], bank.ap[:, :NC], rbc[:, :NC], ALU.mult, [bank.res, rbcr], [kf.res])
            bank = C.mm()
            proj8(bank, wv, 128, xrhs, NC, w.res)
            T.tt(DVE, vT[:, 64:64 + NC], bank.ap[:, :NC], rbc[:, :NC], ALU.mult, [bank.res, rbcr], [vTc])
            if last:
                T.tt(DVE, aux_sb[:, A_VP:A_VP + 128], bank.ap[:, 64:192], rbc[:, 64:192], ALU.mult, [bank.res, rbcr], [auxr])
                T.tt(DVE, aux_sb[:, A_VS + 112:A_VS + 128], bank.ap[:, 208:224], rbc[:, 208:224], ALU.mult, [bank.res, rbcr], [auxr])
            W.release()
            yield
            for half in range(2):
                w = W.next(f"in{3 + half}")
                wv = w.ap.rearrange("p (k n) -> p k n", k=8)
                for t in range(2):
                    gq_ = 2 * half + t
                    bank = C.mm()
                    proj8(bank, wv, t * 128, xrhs, NC, w.res)
                    T.tt(DVE, u[:, gq_, 16:16 + NC], bank.ap[:, :NC], rbc[:, :NC], ALU.mult, [bank.res, rbcr], [uc[gq_]])
                    if last:
                        T.copy(DVE, u[:, gq_, 16 + 192:16 + 208], cach_sb[:, SPT + 16 * gq_:SPT + 16 * gq_ + 16], [cachr], [uc[gq_]])
                W.release()
                yield
            def qk_sq(qf):
                sq = bscr()
                T.act(sq.ap[:, :NC], qf.ap[:, :NC], AF.Square, [qf.res], [sq.res])
                return sq

            def qk_mm(sq, qb):
                T.mm(qb.ap[:, :NC], blk[:, :], sq.ap[:, :NC], True, True, [sq.res, konst], [qb.res])

            def qk_fin(qf, qb, outs, gcol):
                T.act(qb.ap[:, :NC], qb.ap[:, :NC], AF.Ln, [qb.res], [qb.res], scale=1.0 / 64, bias=EPS)
                T.act(qb.ap[:, :NC], qb.ap[:, :NC], AF.Exp, [qb.res], [qb.res], scale=-0.5)
                for (oap, lo, hi, c0, c1, ores) in outs:
                    T.stt(oap, qf.ap[lo:hi, c0:c1], cst_sb[lo:hi, gcol:gcol + 1], qb.ap[lo:hi, c0:c1], ALU.mult, ALU.mult, [qf.res, qb.res, cstr], ores)
            qbanks = C.qk
            sqs = [qk_sq(qf_) for qf_ in qfs]
            qk_mm(sqs[0], qbanks[0])
            sqk = qk_sq(kf)
            for j in range(1, 4):
                qk_mm(sqs[j], qbanks[j])
            qk_mm(sqk, qbanks[4])
            for j in range(4):
                qk_fin(qfs[j], qbanks[j], [(qn[:, j, :NC], 0, 128, 0, NC, [qnr[j]])], C_GQ)
            kouts = [(kT[0][0:64, 128:128 + NC], 0, 64, 0, NC, [kTc]), (kT[1][64:128, 128:128 + NC], 64, 128, 0, NC, [kTc])]
            if last:
                kouts.append((aux_sb[:, A_KP:A_KP + 128], 0, 128, 64, 192, [auxr]))
                kouts.append((aux_sb[:, A_KS + 112:A_KS + 128], 0, 128, 208, 224, [auxr]))
            qk_fin(kf, qbanks[4], kouts, C_GK)
            nA = 4 if not last else 2
            tps = [vTc, vTh, konst]
            for m in range(nA):
                T.tr(tp[:, m * 128:(m + 1) * 128], vT[:, 64 + 128 * m:64 + 128 * (m + 1)], identb[:, :], tps, [tpr[0]])
                T.tr(tp[:, (4 + m) * 128:(5 + m) * 128], vT[:, 128 * m:128 * (m + 1)], identb[:, :], tps, [tpr[0]])
            if last:
                T.tr(tp[:, 2 * 128:3 * 128], vT[:, 64 + 208:64 + 336], identb[:, :], tps, [tpr[0]])
            for m in range(nA):
                TA = 4 * gi + m
                T.copy(ACT, VA[0][:, TA % 6, 0:64], tp[:, m * 128:m * 128 + 64], [tpr[0]], [VAr[TA % 6]])
                T.copy(DVE, VA[1][:, TA % 6, 64:128], tp[:, m * 128 + 64:(m + 1) * 128], [tpr[0]], [VAr[TA % 6]])
                T.copy(ACT, VB[0][:, TA % 4, 0:64], tp[:, (4 + m) * 128:(4 + m) * 128 + 64], [tpr[0]], [VBr[TA % 4]])
                T.copy(DVE, VB[1][:, TA % 4, 64:128], tp[:, (4 + m) * 128 + 64:(5 + m) * 128], [tpr[0]], [VBr[TA % 4]])
            if last:
                T.copy(ACT, Vs[0][:, 0:64], tp[:, 256:320], [tpr[0]], [Vsr])
                T.copy(DVE, Vs[1][:, 64:128], tp[:, 320:384], [tpr[0]], [Vsr])
            L = 16 + NC
            for g in range(4):
                Ug = u[:, g, :]
                cur = Ug
                cres = [uh[g], uc[g]]
                shift = 1
                for stg_ in range(g + 1):
                    nx = pscr()
                    lo = 2 ** (stg_ + 1) - 1
                    T.tt(POOL, nx.ap[:, lo:L], cur[:, lo:L], cur[:, lo - shift:L - shift], ALU.add, cres, [nx.res])
                    cur = nx.ap
                    cres = [nx.res]
                    shift *= 2
                wv_ = float(2 ** (g + 1))
                sc = pscr()
                T.ts(POOL, sc.ap[:, 16:L], cur[:, 16:L], 1.0 / wv_, ALU.mult, cres, [sc.res], s2=0.0, op1=ALU.add)
                if gi == 0:
                    T.tt(POOL, sc.ap[:, 208:224], cur[:, 208:224], cst_sb[:, C_INVC + 16 * g:C_INVC + 16 * g + 16], ALU.mult, cres + [cstr], [sc.res])
                T.tt(POOL, dp[:, g, :NC], sc.ap[:, 16:L], Ug[:, 16:L], ALU.subtract, [sc.res, uc[g]], [dpr[g]])
                if last:
                    T.copy(DVE, aux_sb[:, A_PP + 15 * g:A_PP + 15 * g + 15], Ug[:, 16 + 177:16 + 192], [uc[g]], [auxr])
                    T.copy(DVE, aux_sb[:, A_PS + 15 * g:A_PS + 15 * g + 15], Ug[:, 16 + 209:16 + 224], [uc[g]], [auxr])
            yield
            nch = 8 if not last else 3
            prev = None
            for lc in range(nch):
                c = 8 * gi + lc
                if c < 2:
                    continue
                ev = (c % 2 == 0)
                k0 = lambda g, lc=lc: kT[g][:, 64 * lc:64 * lc + 128]
                if ev:
                    sA = ((c - 2) // 2) % 6
                    v0 = lambda g, sA=sA: VA[g][:, sA, :]
                    v0res = [VAr[sA]]
                else:
                    sB = ((c - 1) // 2) % 4
                    v0 = lambda g, sB=sB: VB[g][:, sB, :]
                    v0res = [VBr[sB]]
                if gi == 0 and c in (3, 4):
                    mb = fscr()
                    mcol = C_NMA if c == 3 else C_NMH
                    T.ts(DVE, mb.ap[:, 0:512], bia_sb[:, 0:512], cst_sb[:, mcol:mcol + 1], ALU.add, [cstr, biar], [mb.res])
                    b0, b0res = mb.ap[:, 0:512], mb.res
                else:
                    b0, b0res = bia_sb[:, 0:512], biar
                if ev:
                    lo, hi = 0, 64
                    k1 = lambda g, lc=lc: kT[g][:, 128 + 64 * lc:128 + 64 * lc + 128]
                    s1A = (c // 2) % 6
                    pb_, pr_ = pT1[0], pT1r[0]
                else:
                    lo, hi = 64, 128
                    k1 = lambda g, lc=lc: kT[g][:, 128 + 64 * (lc - 1):128 + 64 * (lc - 1) + 128]
                    s1A = ((c - 1) // 2) % 6
                    pb_, pr_ = pT1[1], pT1r[1]
                v1 = lambda g, s1A=s1A: VA[g][:, s1A, :]
                st_ = attn_scores(C, 64 * lc, 64, k0, [], v0, v0res, b0, b0res, lo, hi, k1, v1, [VAr[s1A]], bia_sb[:, 512:1024], pb_, pr_)
                if prev is not None:
                    attn_out(prev)
                prev = st_
                yield
            if last:
                st_ = attn_scores(C, 208, 16, lambda g: ckm[g][:, :], [ckvr], lambda g: cvm[g][:, :], [ckvr], bia_sb[:, 1024:1152], biar, 0, 16,
                                  lambda g: kT[g][:, 128 + 208:128 + 336], lambda g: Vs[g][:, :], [Vsr], bia_sb[:, 1152:1280], pT1s, pT1sr)
                attn_out(prev)
                prev = st_
            if prev is not None:
                attn_out(prev)
            w = W.next("poolw")
            wv = w.ap[:, 0:512].rearrange("p (g d) -> p g d", g=4)
            for g in range(4):
                bank = C.mm()
                T.mm(bank.ap[:, :NC], wv[:, g, :], dp[:, g, :NC], True, True, [w.res, dpr[g]], [bank.res])
                T.act(pm[:, g, :NC], bank.ap[:, :NC], AF.Copy, [bank.res, cstr], [pmr[g]], scale=cst_sb[:, C_PSC + g:C_PSC + g + 1])
            W.release()
            if not last:
                for g in range(2):
                    T.copy(POOL, kT[g][:, 0:128], kT[g][:, NC:NC + 128], [kTc], [kTh])
                T.copy(POOL, vT[:, 0:64], vT[:, NC:NC + 64], [vTc], [vTh])
                for g in range(4):
                    T.copy(POOL, u[:, g, 0:16], u[:, g, NC:NC + 16], [uc[g]], [uh[g]])
            yield
            for pp in range(4):
                w = W.next(f"out{pp}")
                wv = w.ap.rearrange("p (k n) -> p k n", k=8)
                for t in range(2):
                    n = 2 * pp + t
                    bank = C.mm()
                    proj8(bank, wv, t * 128, lambda kc: (aT[:, kc, :NC], aTr) if kc < 4 else (pm[:, kc - 4, :NC], pmr[kc - 4]), NC, w.res)
                    T.tt(DVE, X[:, n, :NC], bank.ap[:, :NC], X[:, n, :NC], ALU.add, [bank.res, Xr[n]], [Xr[n]])
                W.release()
                if pp == 1:
                    yield

        def layer1(gi, X, Xr, NC):
            last = gi == 8
            norm(X, Xr, NC, True, True, gbase=C_MIX1)
            xrhs = lambda kc: (xb[:, kc, :NC], xbr[kc])
            wh = None
            pendB = None

            def stageB(f, ch, tb, y0, cw2):
                T.stt(y0.ap[:, :NC], ch.ap[:, 2:2 + NC], cw2, y0.ap[:, :NC], ALU.mult, ALU.add, [ch.res, y0.res, cstr], [y0.res])
                T.tt(DVE, by_ap(f)[:, :NC], y0.ap[:, :NC], tb.ap[:, :NC], ALU.mult, [y0.res, tb.res], [byr[f]])
            for f in range(8):
                wa = W.next(f"ca{f}")
                if f % 2 == 0:
                    wh = W.next(f"ch{f // 2}")
                wav = wa.ap.rearrange("p (k n) -> p k n", k=8)
                whv = wh.ap.rearrange("p (k n) -> p k n", k=8)
                bB, bC, bH = mm6(), mm6(), mm6()
                proj8(bC, wav, 128, xrhs, NC, wa.res)
                proj8(bH, whv, (f % 2) * 128, xrhs, NC, wh.res)
                proj8(bB, wav, 0, xrhs, NC, wa.res)
                if f % 2 == 0:
                    W.release(1)
                else:
                    W.release(2)
                ch = fscr()
                T.copy(POOL, ch.ap[:, 0:2], chh[:, f, :], [chhr[f]], [ch.res])
                T.tt(DVE, ch.ap[:, 2:2 + NC], bC.ap[:, :NC], rr[:, :NC], ALU.mult, [bC.res, rrr], [ch.res])
                T.tt(DVE, ch.ap[:, 2:2 + NC], bH.ap[:, :NC], ch.ap[:, 2:2 + NC], ALU.mult, [bH.res, ch.res], [ch.res])
                tb = fscr()
                T.tt(DVE, tb.ap[:, :NC], bB.ap[:, :NC], rbc[:, :NC], ALU.mult, [bB.res, rbcr], [tb.res])
                if last:
                    T.copy(POOL, ch.ap[:, 2 + 206:2 + 208], cach_sb[:, SCT + 2 * f:SCT + 2 * f + 2], [cachr], [ch.res])
                    T.copy(POOL, aux_sb[:, A_CP + 2 * f:A_CP + 2 * f + 2], ch.ap[:, 2 + 190:2 + 192], [ch.res], [auxr])
                    T.copy(POOL, aux_sb[:, A_CS + 2 * f:A_CS + 2 * f + 2], ch.ap[:, 2 + 222:2 + 224], [ch.res], [auxr])
                else:
                    T.copy(POOL, chh[:, f, :], ch.ap[:, NC:NC + 2], [ch.res], [chhr[f]])
                cw = lambda j, f=f: cst_sb[:, C_CW + 8 * j + f:C_CW + 8 * j + f + 1]
                y0 = fscr()
                T.act(y0.ap[:, :NC], ch.ap[:, 0:NC], AF.Copy, [ch.res, cstr], [y0.res], scale=cw(0))
                y1 = fscr()
                T.act(y1.ap[:, :NC], ch.ap[:, 1:1 + NC], AF.Copy, [ch.res, cstr], [y1.res], scale=cw(1))
                T.tt(POOL, y0.ap[:, :NC], y0.ap[:, :NC], y1.ap[:, :NC], ALU.add, [y0.res, y1.res], [y0.res])
                if pendB is not None:
                    stageB(*pendB)
                pendB = (f, ch, tb, y0, cw(2))
            stageB(*pendB)
            cow = []
            started = []
            for pp in range(3):
                w = W.next(f"co{pp}")
                cow.append(w)
                wv = w.ap.rearrange("p (k n) -> p k n", k=8)
                for t in range(2):
                    bank = mm6()
                    for kc in range(7):
                        T.mm(bank.ap[:, :NC], wv[:, kc, t * 128:(t + 1) * 128], by_ap(kc)[:, :NC], kc == 0, False, [w.res, byr[kc]], [bank.res], signal=False)
                    started.append((bank, wv, t, w, 2 * pp + t))
            for (bank, wv, t, w, n) in started:
                T.mm(bank.ap[:, :NC], wv[:, 7, t * 128:(t + 1) * 128], by_ap(7)[:, :NC], False, True, [w.res, byr[7]], [bank.res], signal=True)
                T.tt(DVE, X[:, n, :NC], bank.ap[:, :NC], X[:, n, :NC], ALU.add, [bank.res, Xr[n]], [Xr[n]])
            W.release(3)
            w = W.next("co3")
            wv = w.ap.rearrange("p (k n) -> p k n", k=8)
            for t in range(2):
                n = 6 + t
                bank = mm6()
                proj8(bank, wv, t * 128, lambda kc: (by_ap(kc)[:, :NC], byr[kc]), NC, w.res)
                T.tt(DVE, X[:, n, :NC], bank.ap[:, :NC], X[:, n, :NC], ALU.add, [bank.res, Xr[n]], [Xr[n]])
            W.release()

        def grp_cols(gi):
            return 512 * gi, (512 if gi < 8 else 224)

        load_x(groups[0])
        l0gen = None
        for idx, gi in enumerate(groups):
            n0, NC = grp_cols(gi)
            s = gi % 2
            X, Xr = xt[s], xr[s]
            if l0gen is None:
                for _ in layer0(gi, X, Xr, NC, CF):
                    pass
            else:
                for _ in l0gen:
                    pass
                l0gen = None
            if stop != "l0":
                has_next = idx + 1 < len(groups)
                pre = (lambda: load_x(groups[idx + 1])) if (has_next and idx > 0) else None
                ffn(X, Xr, NC, "a", on_up=pre)
                if stop != "ffn0":
                    layer1(gi, X, Xr, NC)
                    if stop != "l1":
                        if has_next and idx > 0 and INTERLEAVE:
                            gn = groups[idx + 1]
                            l0gen = layer0(gn, xt[gn % 2], xr[gn % 2], grp_cols(gn)[1], CI)

                            def pull(n, g_=l0gen, cnt=[0]):
                                for _ in range(2 if cnt[0] == 0 else 1):
                                    next(g_, None)
                                cnt[0] += 1
                            ffn(X, Xr, NC, "b", on_down=pull, dring=down_int)
                        else:
                            ffn(X, Xr, NC, "b")
            if idx == 0 and len(groups) > 1:
                assert groups[0] % 2 == 0
                load_x(groups[1])
            T.dma(SP, yT[:, :, n0:n0 + NC], X[:, :, 0:NC], f"s_x{s}", reads=Xr)
        T.dma(SP, aux[:, :], aux_sb[:, :], "s_aux", reads=[auxr])
        for k_, v_ in T.dmacnt.items():
            if v_ and SP.known.get(k_, 0) < v_:
                nc.sync.wait_ge(T.sems[k_], v_)
        if order is None:
            return W.rec
        build.stats = dict(nwaits=T.nwaits, seq={e.name: e.seq for e in (PE, ACT, DVE, POOL)})
    return nc


def _rel_bucket(rel):
    half, max_exact = 16, 8
    n = np.abs(rel)
    large = max_exact + (np.log(np.maximum(n, 1).astype(np.float32) / max_exact) / np.float32(np.log(128 / max_exact)) * (half - max_exact)).astype(np.int32)
    large = np.minimum(large, half - 1)
    return np.where(rel > 0, half, 0) + np.where(n < max_exact, n, large)


def _bucket_table():
    return _rel_bucket(np.arange(-400, 401))


def _pack_weights(inp):
    f32 = np.float32
    out = np.zeros((NP_, 128, PW), f32)

    def pk(Wm, cols):
        return np.transpose(Wm.reshape(8, 128, Wm.shape[1])[:, :, cols], (1, 0, 2))
    Win = inp["ev_w_in"][0]
    tiles = []
    for j in range(4):
        tiles.append(list(range(64 * j, 64 * j + 64)) + list(range(64 * (4 + j), 64 * (4 + j) + 64)))
    tiles.append(list(range(512, 640)))
    tiles.append(list(range(640, 768)))
    for g in range(4):
        tiles.append(list(range(768 + 128 * g, 768 + 128 * (g + 1))))
    rowperm = []
    for kc in range(4):
        rowperm += list(range(64 * kc, 64 * kc + 64)) + list(range(64 * (4 + kc), 64 * (4 + kc) + 64))
    rowperm += list(range(512, 1024))
    Wout = inp["ev_w_out"][0][rowperm]
    idx = {n: i for i, (n, _) in enumerate(PIECES)}
    for i in range(5):
        out[idx[f"in{i}"]] = pk(Win, tiles[2 * i] + tiles[2 * i + 1]).reshape(128, PW)
    out[idx["poolw"], :, 0:512] = np.transpose(inp["pool_w"][0], (1, 0, 2)).reshape(128, 512)
    for i in range(4):
        out[idx[f"out{i}"]] = pk(Wout, list(range(256 * i, 256 * (i + 1)))).reshape(128, PW)
    for l, tag in enumerate("ab"):
        W1 = inp["ffn_w1"][l]
        W2 = inp["ffn_w2"][l]
        for i in range(16):
            out[idx[f"w1{tag}{i}"]] = pk(W1, list(range(256 * i, 256 * (i + 1)))).reshape(128, PW)
        W2r = W2.reshape(32, 128, 1024)
        for n in range(8):
            for hh in range(2):
                blk_ = W2r[16 * hh:16 * (hh + 1), :, 128 * n:128 * (n + 1)]
                out[idx[f"w2{tag}{2 * n + hh}"]] = np.transpose(blk_, (1, 0, 2)).reshape(128, PW)
    Wc = inp["conv_w_in"][0]
    for f in range(8):
        out[idx[f"ca{f}"]] = pk(Wc, list(range(128 * f, 128 * (f + 1))) + list(range(1024 + 128 * f, 1024 + 128 * (f + 1)))).reshape(128, PW)
    for h in range(4):
        out[idx[f"ch{h}"]] = pk(Wc, list(range(2048 + 256 * h, 2048 + 256 * (h + 1)))).reshape(128, PW)
    Wco = inp["conv_w_out"][0]
    for i in range(4):
        out[idx[f"co{i}"]] = pk(Wco, list(range(256 * i, 256 * (i + 1)))).reshape(128, PW)
    return out


def _pack_consts(inp, core):
    f32 = np.float32
    c = np.zeros((128, NCST), f32)
    first = (core % 4 == 0)
    for base, v in ((C_MIX0, inp["norm_mix"][0]), (C_FFN0, inp["norm_ffn"][0]), (C_MIX1, inp["norm_mix"][1]), (C_FFN1, inp["norm_ffn"][1])):
        c[:, base:base + 8] = v.reshape(8, 128).T
    c[:, C_PSC:C_PSC + 4] = inp["pool_scale"][0].reshape(4, 128).T
    c[:, C_PSC + 4] = 1.0
    c[:, C_GQ] = np.tile(inp["q_norm"][0], 2)
    c[:, C_GK] = np.tile(inp["k_norm"][0], 2)
    for j in range(3):
        c[:, C_CW + 8 * j:C_CW + 8 * j + 8] = inp["conv_w"][0][j].reshape(8, 128).T
    sk = inp["attn_sinks"][0]
    c[0:64, C_SINK:C_SINK + 4] = sk[0:4][None, :]
    c[64:128, C_SINK:C_SINK + 4] = sk[4:8][None, :]
    if first:
        c[:, C_NMA] = NEG
        c[0:64, C_NMH] = NEG
    for g in range(4):
        w = 2 ** (g + 1)
        for j in range(16):
            c[:, C_INVC + 16 * g + j] = 1.0 / (min(j + 1, w) if first else w)
    c[:, C_ID:C_ID + 128] = np.eye(128, dtype=f32)
    return c


def _pack_bias(inp):
    f32 = np.float32
    rb = inp["rel_bias"]
    bt = _bucket_table()

    def bk(rel):
        return bt[rel + 400]
    out = np.zeros((128, NBIA), f32)
    k = np.arange(128)[:, None]
    q = np.arange(64)[None, :]
    b0 = rb[bk(k - 128 - q)]
    k64 = (np.arange(128) % 64)[:, None]
    b1 = rb[bk(k64 - q)]

    def lay(b, nq):
        return np.transpose(b.reshape(128, nq, 2, 4), (0, 2, 3, 1)).reshape(128, 8 * nq)
    out[:, 0:512] = lay(b0, 64)
    out[:, 512:1024] = lay(b1, 64)
    qs = np.arange(16)[None, :]
    bs0 = rb[bk((896 + k) - (1024 + qs))]
    k16 = np.minimum(np.arange(128), 15)[:, None]
    bs1 = rb[bk(k16 - qs)]
    out[:, 1024:1152] = lay(bs0, 16)
    out[:, 1152:1280] = lay(bs1, 16)
    return out


_NC_CACHE = {}


def _get_nc():
    if "nc" not in _NC_CACHE:
        _NC_CACHE["nc"] = build()
    return _NC_CACHE["nc"]


def make_in_maps(inp):
    f32 = np.float32
    inp = {k: np.asarray(v) for k, v in inp.items()}
    w32 = _pack_weights(inp)
    bia = _pack_bias(inp)
    maps = []
    for core in range(NCORE):
        b, qd = core // 4, core % 4
        s0 = qd * TOK
        rows = np.zeros((NCOL, 1024), f32)
        lo = s0 - HALO
        if lo >= 0:
            rows[0:HALO + TOK] = inp["x_prompt"][b, lo:s0 + TOK]
        else:
            rows[HALO:HALO + TOK] = inp["x_prompt"][b, s0:s0 + TOK]
        rows[HALO + TOK + 16:] = inp["x_sample"][core]
        xin = np.ascontiguousarray(np.transpose(rows.reshape(NCOL, 8, 128), (2, 1, 0)))
        cach = np.zeros((128, 3 * 128 + 64 + 16), f32)
        ck = inp["cache_k"][0, core].reshape(128, 128)
        cv = inp["cache_v"][0, core].reshape(128, 128)
        cach[:, 0:128] = ck.T
        cach[:, 128:256] = cv.T
        cach[:, 256:384] = cv
        sp = inp["state_pool"][0, core]
        spt = np.zeros((128, 4, 16), f32)
        spt[:, :, 1:16] = np.transpose(sp.reshape(15, 4, 128), (2, 1, 0))
        cach[:, 384:448] = spt.reshape(128, 64)
        sc = inp["state_conv"][0, core]
        cach[:, 448:464] = np.transpose(sc.reshape(2, 8, 128), (2, 1, 0)).reshape(128, 16)
        maps.append({"xin": xin, "w32": w32, "cst": _pack_consts(inp, core), "bia": bia, "cach": cach})
    return maps


def assemble(results):
    f32 = np.float32
    y_p = np.zeros((2, 16384, 1024), f32)
    y_s = np.zeros((8, 16, 1024), f32)
    k_p = np.zeros((1, 2, 128, 2, 64), f32); v_p = np.zeros_like(k_p)
    pool_p = np.zeros((1, 2, 15, 512), f32); conv_p = np.zeros((1, 2, 2, 1024), f32)
    k_s = np.zeros((1, 8, 128, 2, 64), f32); v_s = np.zeros_like(k_s)
    pool_s = np.zeros((1, 8, 15, 512), f32); conv_s = np.zeros((1, 8, 2, 1024), f32)
    for core in range(NCORE):
        r = results[core]
        yT = np.asarray(r["yT"]); ax = np.asarray(r["aux"])
        b, qd = core // 4, core % 4
        y_p[b, qd * TOK:(qd + 1) * TOK] = np.transpose(yT[:, :, HALO:HALO + TOK], (2, 1, 0)).reshape(TOK, 1024)
        y_s[core] = np.transpose(yT[:, :, HALO + TOK + 16:], (2, 1, 0)).reshape(16, 1024)
        if qd == 3:
            k_p[0, b] = ax[:, A_KP:A_KP + 128].T.reshape(128, 2, 64)
            v_p[0, b] = ax[:, A_VP:A_VP + 128].T.reshape(128, 2, 64)
            pool_p[0, b] = np.transpose(ax[:, A_PP:A_PP + 60].reshape(128, 4, 15), (2, 1, 0)).reshape(15, 512)
            conv_p[0, b] = np.transpose(ax[:, A_CP:A_CP + 16].reshape(128, 8, 2), (2, 1, 0)).reshape(2, 1024)
        k_s[0, core] = ax[:, A_KS:A_KS + 128].T.reshape(128, 2, 64)
        v_s[0, core] = ax[:, A_VS:A_VS + 128].T.reshape(128, 2, 64)
        pool_s[0, core] = np.transpose(ax[:, A_PS:A_PS + 60].reshape(128, 4, 15), (2, 1, 0)).reshape(15, 512)
        conv_s[0, core] = np.transpose(ax[:, A_CS:A_CS + 16].reshape(128, 8, 2), (2, 1, 0)).reshape(2, 1024)
    return (y_p, y_s, k_p, v_p, pool_p, conv_p, k_s, v_s, pool_s, conv_s)


def kernel(**inputs):
    nc = _get_nc()
    maps = make_in_maps(inputs)
    res = run_bass_kernel_spmd(nc, maps, core_ids=list(range(NCORE)))
    return assemble(res.results)
```
